# Optimizing a Trainium2 kernel written in Bass

```python
import math
import jax, jax.numpy as jnp
from jax import lax
import numpy as np

D_MODEL = 1024
BATCH = 4
SEQ = 8192
DEPTH = 2
DEC_BATCH = 2
DEC_SEQ = 8192
PAST_LEN = 128

GRID_W = 64
PLE_DIM = 256
EPS = 1e-6
SSD_HEADS = 32
SSD_HEADDIM = 64
D_INNER = SSD_HEADS * SSD_HEADDIM
SSD_GROUPS = 4
SSD_STATE = 128
SSD_CHUNK = 128
CONV_W = 5
GN = SSD_GROUPS * SSD_STATE
XBC_DIM = D_INNER + 2 * GN
HEAD_DIM = 128
GQA_Q_HEADS = 16
GQA_KV_HEADS = 4
GQA_REP = GQA_Q_HEADS // GQA_KV_HEADS
GQA_WIDTH = GQA_Q_HEADS * HEAD_DIM
GQA_KV_WIDTH = GQA_KV_HEADS * HEAD_DIM
Q_BLOCK = 128
ROPE_THETA = 10000.0
ATTN_SCALE = HEAD_DIM ** -0.5
DIL_PATTERNS = ((128, 1), (512, 4), (2048, 16))
DIL_HEADS_PER_GROUP = 4
DIL_HEADS = len(DIL_PATTERNS) * DIL_HEADS_PER_GROUP
DIL_WIDTH = DIL_HEADS * HEAD_DIM
DIL_OUT = DIL_HEADS_PER_GROUP * HEAD_DIM
N_BUCKETS = 32
REL_MAX_DIST = 2048
FFN_DIM = ((8 * D_MODEL // 3 + 255) // 256) * 256
N_BRANCHES = 3
IN_WIDTHS = (D_INNER, XBC_DIM, 2 * SSD_HEADS, GQA_WIDTH, GQA_KV_WIDTH, GQA_KV_WIDTH,
             DIL_WIDTH, DIL_WIDTH, DIL_WIDTH, N_BRANCHES * D_MODEL)
IN_TOTAL = sum(IN_WIDTHS)

kernel_name = "hybrid_gated_ssd_gqa_dilated_encoder"


def rmsnorm(x, g):
    xf = x.astype(jnp.float32)
    y = xf * lax.rsqrt(jnp.mean(xf * xf, axis=-1, keepdims=True) + EPS)
    return (y * g.astype(jnp.float32)).astype(x.dtype)


def depthwise_conv(x, w):
    pad = CONV_W // 2
    return lax.conv_general_dilated(
        x, w[:, None, :].astype(x.dtype), window_strides=(1,), padding=[(pad, pad)],
        dimension_numbers=('NWC', 'WIO', 'NWC'), feature_group_count=x.shape[-1])


def ssd_scan(xs, dt, a, bm, cm):
    b, s = xs.shape[:2]
    nc = s // SSD_CHUNK
    hg = SSD_HEADS // SSD_GROUPS
    shp = (b, nc, SSD_CHUNK, SSD_GROUPS)
    xc = (xs * dt[..., None]).reshape(*shp, hg, SSD_HEADDIM)
    acs = jnp.cumsum((dt * a).reshape(*shp, hg), axis=2)
    bc = bm.reshape(*shp, SSD_STATE)
    cc = cm.reshape(*shp, SSD_STATE)
    causal = jnp.tril(jnp.ones((SSD_CHUNK, SSD_CHUNK), dtype=bool))
    seg = acs[:, :, :, None] - acs[:, :, None, :]
    decay = jnp.exp(jnp.where(causal[:, :, None, None], seg, -jnp.inf))
    cb = jnp.einsum('bclgn,bcsgn->bclsg', cc, bc)
    y_diag = jnp.einsum('bclsg,bclsgh,bcsghp->bclghp', cb, decay, xc)
    decay_end = jnp.exp(acs[:, :, -1:] - acs)
    states = jnp.einsum('bclgn,bclgh,bclghp->bcghpn', bc, decay_end, xc)
    chunk_decay = jnp.exp(acs[:, :, -1])

    def step(h_prev, inp):
        st, dec = inp
        return h_prev * dec[..., None, None] + st, h_prev

    init = jnp.zeros_like(states[:, 0])
    _, h_in = lax.scan(step, init, (jnp.moveaxis(states, 1, 0), jnp.moveaxis(chunk_decay, 1, 0)))
    h_in = jnp.moveaxis(h_in, 0, 1)
    y_off = jnp.einsum('bclgn,bcghpn,bclgh->bclghp', cc, h_in, jnp.exp(acs))
    return (y_diag + y_off).reshape(b, s, SSD_HEADS, SSD_HEADDIM)


def flip_seq(t):
    return jnp.flip(t, axis=1)


def ssd_branch(z, xbc, dt_raw, conv_w, conv_b, dt_bias, a_log, d_skip, g_norm):
    b, s, _ = z.shape
    xbc = jax.nn.silu(depthwise_conv(xbc, conv_w) + conv_b).astype(jnp.float32)
    xs = xbc[..., :D_INNER].reshape(b, s, SSD_HEADS, SSD_HEADDIM)
    bm = xbc[..., D_INNER:D_INNER + GN].reshape(b, s, SSD_GROUPS, SSD_STATE)
    cm = xbc[..., D_INNER + GN:].reshape(b, s, SSD_GROUPS, SSD_STATE)
    dt = jax.nn.softplus(dt_raw.astype(jnp.float32).reshape(b, s, 2, SSD_HEADS)
                         + dt_bias.astype(jnp.float32))
    a = -jnp.exp(a_log.astype(jnp.float32))
    y_fwd = ssd_scan(xs, dt[:, :, 0], a[0], bm, cm)
    y_bwd = flip_seq(ssd_scan(flip_seq(xs), flip_seq(dt[:, :, 1]), a[1], flip_seq(bm), flip_seq(cm)))
    y = y_fwd + y_bwd + xs * d_skip.astype(jnp.float32)[:, None]
    y = y.reshape(b, s, D_INNER) * jax.nn.silu(z.astype(jnp.float32))
    return rmsnorm(y, g_norm).astype(z.dtype)


def axial_rope(rows):
    row = jnp.repeat(jnp.arange(rows), GRID_W).astype(jnp.float32)
    col = jnp.tile(jnp.arange(GRID_W), rows).astype(jnp.float32)
    n_pairs = HEAD_DIM // 4
    inv = ROPE_THETA ** (-jnp.arange(n_pairs, dtype=jnp.float32) / n_pairs)
    ang = jnp.concatenate([row[:, None] * inv, col[:, None] * inv], axis=-1)
    return jnp.cos(ang), jnp.sin(ang)


def apply_rope(x, cos, sin):
    xf = x.astype(jnp.float32).reshape(*x.shape[:-1], HEAD_DIM // 2, 2)
    x0, x1 = xf[..., 0], xf[..., 1]
    c = cos[None, :, None, :]
    s = sin[None, :, None, :]
    out = jnp.stack([x0 * c - x1 * s, x0 * s + x1 * c], axis=-1).reshape(x.shape)
    return out.astype(x.dtype)


def gqa_attention(q, k, v):
    b, s = q.shape[:2]
    nb = s // Q_BLOCK
    qb = q.reshape(b, nb, Q_BLOCK, GQA_KV_HEADS, GQA_REP, HEAD_DIM).transpose(1, 0, 2, 3, 4, 5)

    def block(qblk):
        logits = jnp.einsum('bqkrd,bskd->bkrqs', qblk, k).astype(jnp.float32) * ATTN_SCALE
        probs = jax.nn.softmax(logits, axis=-1).astype(v.dtype)
        return jnp.einsum('bkrqs,bskd->bqkrd', probs, v)

    out = lax.map(block, qb)
    return out.transpose(1, 0, 2, 3, 4, 5).reshape(b, s, GQA_WIDTH)


def t5_bucket(rel):
    nb = N_BUCKETS // 2
    max_exact = nb // 2
    ret = jnp.where(rel > 0, nb, 0)
    n = jnp.abs(rel)
    nf = jnp.maximum(n, 1).astype(jnp.float32)
    large = max_exact + (jnp.log(nf / max_exact) / math.log(REL_MAX_DIST / max_exact)
                         * (nb - max_exact)).astype(jnp.int32)
    large = jnp.minimum(large, nb - 1)
    return ret + jnp.where(n < max_exact, n, large)


def dilated_biases(rel_bias):
    out = []
    for g, (window, dil) in enumerate(DIL_PATTERNS):
        half = window // (2 * dil)
        dist = jnp.arange(-half, half + 1, dtype=jnp.int32) * dil
        tbl = rel_bias[t5_bucket(dist)]
        out.append(tbl[:, g * DIL_HEADS_PER_GROUP:(g + 1) * DIL_HEADS_PER_GROUP].T)
    return out


def to_residue(t, d):
    b, s = t.shape[:2]
    rest = t.shape[2:]
    return t.reshape(b, s // d, d, *rest).swapaxes(1, 2).reshape(b * d, s // d, *rest)


def from_residue(t, d, b):
    l = t.shape[1]
    rest = t.shape[2:]
    return t.reshape(b, d, l, *rest).swapaxes(1, 2).reshape(b, l * d, *rest)


def banded_attention(q, k, v, bias, half):
    n, l, h, dh = q.shape
    blk = half
    nb = -(-l // blk)
    lp = nb * blk
    qp = jnp.pad(q, ((0, 0), (0, lp - l), (0, 0), (0, 0))).reshape(n, nb, blk, h, dh)

    def windows(t):
        tp = jnp.pad(t, ((0, 0), (blk, lp - l + blk), (0, 0), (0, 0))).reshape(n, nb + 2, blk, h, dh)
        return jnp.concatenate([tp[:, :-2], tp[:, 1:-1], tp[:, 2:]], axis=2)

    kw, vw = windows(k), windows(v)
    qpos = jnp.arange(lp).reshape(nb, blk)
    kpos = jnp.arange(nb)[:, None] * blk - blk + jnp.arange(3 * blk)[None, :]
    rel = kpos[:, None, :] - qpos[:, :, None]
    inside = (kpos >= 0) & (kpos < l)
    valid = ((jnp.abs(rel) <= half) & inside[:, None, :]) | (rel == 0)
    bias_blk = jnp.take(bias.astype(jnp.float32), jnp.clip(rel + half, 0, 2 * half), axis=1)
    logits = (jnp.einsum('nbqhd,nbkhd->nbhqk', qp, kw).astype(jnp.float32) * ATTN_SCALE
              + jnp.moveaxis(bias_blk, 0, 1)[None])
    logits = jnp.where(valid[None, :, None], logits, -jnp.inf)
    lse = jax.nn.logsumexp(logits, axis=-1)
    probs = jnp.exp(logits - lse[..., None]).astype(v.dtype)
    out = jnp.einsum('nbhqk,nbkhd->nbqhd', probs, vw).reshape(n, lp, h, dh)[:, :l]
    lse = jnp.moveaxis(lse, 2, 3).reshape(n, lp, h)[:, :l]
    return out, lse


def dilated_attention(q, k, v, biases):
    b, s = q.shape[:2]
    outs, lses = [], []
    for g, (window, dil) in enumerate(DIL_PATTERNS):
        hs = slice(g * DIL_HEADS_PER_GROUP, (g + 1) * DIL_HEADS_PER_GROUP)
        o, lse = banded_attention(to_residue(q[:, :, hs], dil), to_residue(k[:, :, hs], dil),
                                  to_residue(v[:, :, hs], dil), biases[g], window // (2 * dil))
        outs.append(from_residue(o, dil, b))
        lses.append(from_residue(lse, dil, b))
    wts = jax.nn.softmax(jnp.stack(lses), axis=0)
    out = jnp.sum(wts[..., None] * jnp.stack(outs).astype(jnp.float32), axis=0)
    return out.reshape(b, s, DIL_OUT).astype(q.dtype)


def encoder_layer(x, p_i, i, P, cos, sin, dil_bias):
    b, s, _ = x.shape
    h = rmsnorm(x, P['g_pre_mix'][i])
    proj = h @ P['w_in'][i]
    split_at = [int(c) for c in np.cumsum(IN_WIDTHS)[:-1]]
    z, xbc, dt_raw, gq, gk, gv, dq, dk, dv, gates = jnp.split(proj, split_at, axis=-1)
    y_ssd = ssd_branch(z, xbc, dt_raw, P['conv_w'][i], P['conv_b'][i], P['dt_bias'][i],
                       P['a_log'][i], P['d_skip'][i], P['g_ssd'][i])
    q = apply_rope(rmsnorm(gq.reshape(b, s, GQA_Q_HEADS, HEAD_DIM), P['g_q'][i]), cos, sin)
    k = apply_rope(rmsnorm(gk.reshape(b, s, GQA_KV_HEADS, HEAD_DIM), P['g_k'][i]), cos, sin)
    y_gqa = gqa_attention(q, k, gv.reshape(b, s, GQA_KV_HEADS, HEAD_DIM))
    y_dil = dilated_attention(dq.reshape(b, s, DIL_HEADS, HEAD_DIM), dk.reshape(b, s, DIL_HEADS, HEAD_DIM),
                              dv.reshape(b, s, DIL_HEADS, HEAD_DIM), dil_bias)
    br = jnp.stack([y_ssd @ P['w_br_ssd'][i], y_gqa @ P['w_br_gqa'][i], y_dil @ P['w_br_dil'][i]], axis=2)
    g = jax.nn.sigmoid(gates.reshape(b, s, N_BRANCHES, D_MODEL))
    mix = jnp.sum(g * br, axis=2) @ P['w_out'][i]
    x = x + rmsnorm(mix, P['g_post_mix'][i])
    h = rmsnorm(x, P['g_pre_ffn'][i])
    ff = (jax.nn.silu(h @ P['w_gate'][i]) * (h @ P['w_up'][i])) @ P['w_down'][i]
    x = x + rmsnorm(ff, P['g_post_ffn'][i])
    ple_gate = jax.nn.sigmoid(rmsnorm(x, P['g_ple'][i]) @ P['w_ple_gate'][i])
    return x + (p_i @ P['w_ple'][i]) * ple_gate


def trunk(x, p, P, rel_bias):
    rows = x.shape[1] // GRID_W
    cos, sin = axial_rope(rows)
    dil_bias = dilated_biases(rel_bias)
    for i in range(DEPTH):
        x = encoder_layer(x, p[i], i, P, cos, sin, dil_bias)
    return x


def setup_inputs(seed: int = 0) -> dict:
    key = jax.random.key(seed)
    ks = jax.random.split(key, 28)
    f32 = jnp.float32

    def nrm(k, shape, fan_in):
        return jax.random.normal(k, shape, f32) * fan_in ** -0.5

    def gain(k, shape):
        return 1.0 + 0.02 * jax.random.normal(k, shape, f32)

    dt = jnp.exp(jax.random.uniform(ks[7], (DEPTH, 2, SSD_HEADS), f32, math.log(1e-3), math.log(1e-1)))
    return {
        'x_prompt': jax.random.normal(ks[0], (BATCH, SEQ, D_MODEL), f32),
        'x_sample': jax.random.normal(ks[1], (DEC_BATCH, DEC_SEQ, D_MODEL), f32),
        'p_prompt': jax.random.normal(ks[2], (DEPTH, BATCH, SEQ, PLE_DIM), f32),
        'p_sample': jax.random.normal(ks[3], (DEPTH, DEC_BATCH, DEC_SEQ, PLE_DIM), f32),
        'w_in': nrm(ks[4], (DEPTH, D_MODEL, IN_TOTAL), D_MODEL),
        'conv_w': nrm(ks[5], (DEPTH, CONV_W, XBC_DIM), CONV_W),
        'conv_b': 0.01 * jax.random.normal(ks[6], (DEPTH, XBC_DIM), f32),
        'dt_bias': dt + jnp.log(-jnp.expm1(-dt)),
        'a_log': jnp.log(jax.random.uniform(ks[8], (DEPTH, 2, SSD_HEADS), f32, 1.0, 16.0)),
        'd_skip': gain(ks[9], (DEPTH, SSD_HEADS)),
        'g_ssd': gain(ks[10], (DEPTH, D_INNER)),
        'g_q': gain(ks[11], (DEPTH, HEAD_DIM)),
        'g_k': gain(ks[12], (DEPTH, HEAD_DIM)),
        'w_br_ssd': nrm(ks[13], (DEPTH, D_INNER, D_MODEL), D_INNER),
        'w_br_gqa': nrm(ks[14], (DEPTH, GQA_WIDTH, D_MODEL), GQA_WIDTH),
        'w_br_dil': nrm(ks[15], (DEPTH, DIL_OUT, D_MODEL), DIL_OUT),
        'w_out': nrm(ks[16], (DEPTH, D_MODEL, D_MODEL), D_MODEL),
        'g_pre_mix': gain(ks[17], (DEPTH, D_MODEL)),
        'g_post_mix': gain(ks[18], (DEPTH, D_MODEL)),
        'g_pre_ffn': gain(ks[19], (DEPTH, D_MODEL)),
        'g_post_ffn': gain(ks[20], (DEPTH, D_MODEL)),
        'w_gate': nrm(ks[21], (DEPTH, D_MODEL, FFN_DIM), D_MODEL),
        'w_up': nrm(ks[22], (DEPTH, D_MODEL, FFN_DIM), D_MODEL),
        'w_down': nrm(ks[23], (DEPTH, FFN_DIM, D_MODEL), FFN_DIM),
        'w_ple': nrm(ks[24], (DEPTH, PLE_DIM, D_MODEL), PLE_DIM),
        'g_ple': gain(ks[25], (DEPTH, D_MODEL)),
        'w_ple_gate': nrm(ks[26], (DEPTH, D_MODEL, D_MODEL), D_MODEL),
        'rel_bias': 0.1 * jax.random.normal(ks[27], (N_BUCKETS, DIL_HEADS), f32),
    }


def reference(x_prompt, x_sample, p_prompt, p_sample, w_in, conv_w, conv_b, dt_bias, a_log, d_skip,
              g_ssd, g_q, g_k, w_br_ssd, w_br_gqa, w_br_dil, w_out, g_pre_mix, g_post_mix,
              g_pre_ffn, g_post_ffn, w_gate, w_up, w_down, w_ple, g_ple, w_ple_gate, rel_bias):
    P = dict(w_in=w_in, conv_w=conv_w, conv_b=conv_b, dt_bias=dt_bias, a_log=a_log, d_skip=d_skip,
             g_ssd=g_ssd, g_q=g_q, g_k=g_k, w_br_ssd=w_br_ssd, w_br_gqa=w_br_gqa, w_br_dil=w_br_dil,
             w_out=w_out, g_pre_mix=g_pre_mix, g_post_mix=g_post_mix, g_pre_ffn=g_pre_ffn,
             g_post_ffn=g_post_ffn, w_gate=w_gate, w_up=w_up, w_down=w_down, w_ple=w_ple,
             g_ple=g_ple, w_ple_gate=w_ple_gate)
    y_prompt = trunk(x_prompt, p_prompt, P, rel_bias)
    y_sample = trunk(x_sample, p_sample, P, rel_bias)
    return (y_prompt, y_sample)
```

```python
import math
import numpy as np
import concourse.bass as bass
import concourse.mybir as mybir
from concourse.bass_utils import run_bass_kernel_spmd

F32 = mybir.dt.float32
BF16 = mybir.dt.bfloat16
AF = mybir.ActivationFunctionType
ALU = mybir.AluOpType
AX = mybir.AxisListType

D = 1024
KD = 8
DEPTH = 2
PLE = 256
EPS = 1e-6
NH = 32
HP = 64
DI = 2048
GN = 512
XBC = 3072
HD = 128
QW = 2048
KVW = 512
DW = 1536
DOUT = 512
FF = 2816
FT = 22
IN_TOTAL = 15936
O_Z, O_XBC, O_DT, O_GQ, O_GK, O_GV, O_DQ, O_DK, O_DV, O_GT = 0, 2048, 5120, 5184, 7232, 7744, 8256, 9792, 11328, 12864
ATT_SCALE = HD ** -0.5
NEG = -30000.0
DILS = (1, 4, 16)

DEBUG_STOP = 0
SEM_MAX = 30000
DSEM_MAX = 1800


class Ev:
    __slots__ = ("sem", "val", "key", "eng")

    def __init__(self, sem, val, key, eng=None):
        self.sem, self.val, self.key, self.eng = sem, val, key, eng


class Tile:
    def __init__(self, h, name):
        self.h, self.name = h, name
        self.w = None
        self.r = []
        self.pend = {}

    def __getitem__(self, idx):
        return TV(self, self.h[idx])


class TV:
    __slots__ = ("tile", "ap")

    def __init__(self, tile, ap):
        self.tile, self.ap = tile, ap

    def rr(self, s, **kw):
        return TV(self.tile, self.ap.rearrange(s, **kw))

    def mod(self, dim, stride=None, count=None, off=0):
        a = [list(x) for x in self.ap.ap]
        if stride is not None:
            a[dim][0] = stride
        if count is not None:
            a[dim][1] = count
        return TV(self.tile, bass.AP(tensor=self.ap.tensor, offset=self.ap.offset + off, ap=a))

    def bc(self, dim, n):
        return self.mod(dim, 0, n)


def _ap(x):
    return x.ap if isinstance(x, TV) else x


class Eng:
    def __init__(self, K, name, e):
        self.K, self.name, self.e = K, name, e
        self.sem = None
        self.cnt = 0
        self.epoch = 0
        self.waited = {}
        self.pending = []
        self.last = None

    def wait(self, ev):
        if ev is None:
            return
        if self.waited.get(ev.key, 0) >= ev.val:
            return
        self.e.wait_ge(ev.sem, ev.val)
        self.waited[ev.key] = ev.val

    def signal(self, ins):
        if self.sem is None or self.cnt >= SEM_MAX:
            self.sem = self.K.nc.alloc_semaphore(f"s_{self.name}_{self.epoch}")
            self.epoch += 1
            self.cnt = 0
        self.cnt += 1
        ins.then_inc(self.sem, 1)
        ev = Ev(self.sem, self.cnt, (self.name, self.epoch), self)
        self.last = ev
        return ev


class DmaQ:
    def __init__(self, K, eng, nslots=8, name="q"):
        self.K, self.eng, self.name = K, eng, name
        self.nslots = nslots
        self.slots = [[None, 0, 0] for _ in range(nslots)]
        self.n = 0
        self.outstanding = [None] * nslots

    def issue(self, make):
        k = self.n % self.nslots
        self.n += 1
        sl = self.slots[k]
        if self.outstanding[k] is not None:
            self.eng.wait(self.outstanding[k])
        if sl[0] is None or sl[1] >= DSEM_MAX:
            sl[0] = self.K.nc.alloc_semaphore(f"d_{self.name}_{k}_{sl[2]}")
            sl[2] += 1
            sl[1] = 0
        sl[1] += 1
        ins = make()
        ins.then_inc(sl[0], 16)
        ev = Ev(sl[0], 16 * sl[1], ("d", self.name, k, sl[2]))
        self.outstanding[k] = ev
        return ev


class Kern:
    def __init__(self, nc):
        self.nc = nc
        self.pe = Eng(self, "pe", nc.tensor)
        self.act = Eng(self, "act", nc.scalar)
        self.dve = Eng(self, "dve", nc.vector)
        self.pool = Eng(self, "pool", nc.gpsimd)
        self.sp = Eng(self, "sp", nc.sync)
        self.engs = [self.pe, self.act, self.dve, self.pool, self.sp]
        self.ldq = DmaQ(self, self.sp, 12, "ld")
        self.stq = DmaQ(self, self.sp, 12, "st")
        self.uid = 0
        self.ninstr = 0

    def sb(self, stack, name, shape, dt):
        self.uid += 1
        h = stack.enter_context(self.nc.sbuf_tensor(f"{name}_{self.uid}", list(shape), dt))
        return Tile(h, name)

    def ps(self, stack, name, shape, dt=F32):
        nbytes = int(np.prod(shape[1:])) * (4 if dt == F32 else 2)
        assert nbytes % 2048 == 0, f"PSUM tile {name} must cover whole banks (collisions are HW errors)"
        self.uid += 1
        h = stack.enter_context(self.nc.psum_tensor(f"{name}_{self.uid}", list(shape), dt))
        return Tile(h, name)

    def _deps(self, eng, reads, writes):
        evs = []
        for t in reads:
            for en, n in t.pend.items():
                if n and en is not eng:
                    raise RuntimeError(f"pending unsignaled access on {t.name} by {en.name}")
            if t.w is not None and not (t.w.eng is eng and eng is self.pe):
                evs.append(t.w)
        for t in writes:
            for en, n in t.pend.items():
                if n and en is not eng:
                    raise RuntimeError(f"pending unsignaled access on {t.name} by {en.name}")
            if t.w is not None and t.w.eng is not eng:
                evs.append(t.w)
            evs.extend(e for e in t.r if e.eng is not eng)
        for ev in evs:
            eng.wait(ev)

    def _record(self, ev, reads, writes):
        for t in reads:
            t.r.append(ev)
            if len(t.r) > 24:
                t.r = t.r[-24:] if False else self._compact(t.r)
        for t in writes:
            t.w = ev
            t.r = []

    @staticmethod
    def _compact(evs):
        best = {}
        for ev in evs:
            if ev.key not in best or best[ev.key].val < ev.val:
                best[ev.key] = ev
        return list(best.values())

    def op(self, eng, fn, outs, ins, sig=True):
        reads = [x.tile for x in ins if isinstance(x, TV)]
        writes = [x.tile for x in outs if isinstance(x, TV)]
        self._deps(eng, reads, writes)
        ins_ = fn()
        self.ninstr += 1
        if sig:
            ev = eng.signal(ins_)
            for (t, kind) in eng.pending:
                t.pend[eng] -= 1
                if kind == "r":
                    t.r.append(ev)
                else:
                    t.w = ev
                    t.r = []
            eng.pending = []
            self._record(ev, reads, writes)
            return ev
        for t in reads:
            eng.pending.append((t, "r"))
            t.pend[eng] = t.pend.get(eng, 0) + 1
        for t in writes:
            eng.pending.append((t, "w"))
            t.pend[eng] = t.pend.get(eng, 0) + 1
        return None

    def dma(self, out, in_, q=None, **kw):
        q = q or (self.stq if not isinstance(out, TV) else self.ldq)
        eng = q.eng
        reads = [in_.tile] if isinstance(in_, TV) else []
        writes = [out.tile] if isinstance(out, TV) else []
        self._deps(eng, reads, writes)
        ev = q.issue(lambda: eng.e.dma_start(out=_ap(out), in_=_ap(in_), **kw))
        self.ninstr += 1
        self._record(ev, reads, writes)
        return ev

    def barrier(self):
        evs = []
        for q in (self.ldq, self.stq):
            evs.extend([e for e in q.outstanding if e is not None])
        for en in self.engs:
            if en.pending:
                raise RuntimeError("pending at barrier " + en.name)
            if en.last is not None:
                evs.append(en.last)
        evs = self._compact(evs)
        for en in self.engs:
            for ev in evs:
                en.wait(ev)

    def mm(self, out, lhsT, rhs, start=True, stop=True, sig=True):
        return self.op(self.pe, lambda: self.nc.tensor.matmul(_ap(out), _ap(lhsT), _ap(rhs), start=start, stop=stop),
                       [out], [lhsT, rhs], sig=sig)

    def tr(self, out, in_, ident, sig=True):
        return self.op(self.pe, lambda: self.nc.tensor.transpose(_ap(out), _ap(in_), _ap(ident)),
                       [out], [in_, ident], sig=sig)

    def actf(self, out, in_, func, bias=None, scale=None, accum=None):
        kw = {}
        ins = [in_]
        outs = [out]
        if bias is not None:
            kw["bias"] = _ap(bias)
            if isinstance(bias, TV):
                ins.append(bias)
        if scale is not None:
            kw["scale"] = _ap(scale)
            if isinstance(scale, TV):
                ins.append(scale)
        if accum is not None:
            kw["accum_out"] = _ap(accum)
            outs.append(accum)
        return self.op(self.act, lambda: self.nc.scalar.activation(_ap(out), _ap(in_), func, **kw), outs, ins)

    def _veng(self, eng):
        return eng or self.dve

    def tt(self, out, a, b, op, eng=None):
        eng = self._veng(eng)
        return self.op(eng, lambda: eng.e.tensor_tensor(_ap(out), _ap(a), _ap(b), op), [out], [a, b])

    def ts(self, out, a, s1, s2, op0, op1=None, eng=None):
        eng = self._veng(eng)
        ins = [a] + [s for s in (s1, s2) if isinstance(s, TV)]
        if op1 is None:
            return self.op(eng, lambda: eng.e.tensor_scalar(_ap(out), _ap(a), _ap(s1), None, op0), [out], ins)
        return self.op(eng, lambda: eng.e.tensor_scalar(_ap(out), _ap(a), _ap(s1), _ap(s2), op0, op1), [out], ins)

    def stt(self, out, a, s, b, op0, op1):
        ins = [a, b] + ([s] if isinstance(s, TV) else [])
        return self.op(self.dve, lambda: self.nc.vector.scalar_tensor_tensor(_ap(out), _ap(a), _ap(s), _ap(b), op0, op1),
                       [out], ins)

    def cp(self, out, in_, eng=None):
        eng = self._veng(eng)
        if eng is self.act:
            return self.op(eng, lambda: self.nc.scalar.copy(_ap(out), _ap(in_)), [out], [in_])
        return self.op(eng, lambda: eng.e.tensor_copy(_ap(out), _ap(in_)), [out], [in_])

    def recip(self, out, in_):
        return self.op(self.dve, lambda: self.nc.vector.reciprocal(_ap(out), _ap(in_)), [out], [in_])

    def rsum(self, out, in_):
        return self.op(self.dve, lambda: self.nc.vector.reduce_sum(_ap(out), _ap(in_), AX.X), [out], [in_])

    def mset(self, out, val, eng=None):
        eng = self._veng(eng)
        return self.op(eng, lambda: eng.e.memset(_ap(out), val), [out], [])


C_ID, C_U, C_LS, C_L, C_US, C_ONE, C_BT = 0, 128, 256, 384, 512, 640, 768
CW = 768 + 3 * 384


def _t5_bucket_np(rel):
    nb = 16
    max_exact = 8
    ret = np.where(rel > 0, nb, 0)
    n = np.abs(rel)
    nf = np.maximum(n, 1).astype(np.float32)
    large = max_exact + (np.log(nf / np.float32(max_exact)) / np.float32(math.log(2048 / max_exact))
                         * np.float32(nb - max_exact)).astype(np.int32)
    large = np.minimum(large, nb - 1)
    return ret + np.where(n < max_exact, n, large)


def make_consts(S):
    c = np.zeros((128, CW), np.float32)
    p = np.arange(128)[:, None]
    j = np.arange(128)[None, :]
    c[:, C_ID:C_ID + 128] = (p == j)
    c[:, C_U:C_U + 128] = (p <= j)
    c[:, C_LS:C_LS + 128] = (p > j)
    c[:, C_L:C_L + 128] = (p >= j)
    c[:, C_US:C_US + 128] = (p < j)
    c[:, C_ONE:C_ONE + 128] = 1.0
    for g, dd in enumerate(DILS):
        k = np.arange(128)[:, None, None]
        o = np.arange(3)[None, :, None]
        q = np.arange(128)[None, None, :]
        rel = k + 128 * (o - 1) - q
        bk = _t5_bucket_np(rel * dd).astype(np.float32)
        bk = np.where(np.abs(rel) <= 64, bk, -1.0)
        c[:, C_BT + g * 384:C_BT + (g + 1) * 384] = bk.reshape(128, 384)
    t = np.arange(S)
    row = (t // 64).astype(np.float32)
    col = (t % 64).astype(np.float32)
    inv = (np.float32(10000.0) ** (-np.arange(32, dtype=np.float32) / np.float32(32))).astype(np.float32)
    ang = np.concatenate([row[:, None] * inv, col[:, None] * inv], axis=-1).astype(np.float32)
    cs = np.concatenate([np.cos(ang), np.sin(ang)], axis=-1).astype(np.float32)
    return c, cs


WSPECS = [("w_in", D, IN_TOTAL), ("w_br_ssd", DI, D), ("w_br_gqa", QW, D), ("w_br_dil", DOUT, D), ("w_out", D, D),
          ("w_gate", D, FF), ("w_up", D, FF), ("w_down", FF, D), ("w_ple", PLE, D), ("w_ple_gate", D, D)]
VSPECS = [("conv_w", [DEPTH, 5, XBC]), ("conv_b", [DEPTH, XBC]), ("dt_bias", [DEPTH, 2, NH]), ("a_log", [DEPTH, 2, NH]),
          ("d_skip", [DEPTH, NH]), ("g_ssd", [DEPTH, DI]), ("g_q", [DEPTH, HD]), ("g_k", [DEPTH, HD]),
          ("g_pre_mix", [DEPTH, D]), ("g_post_mix", [DEPTH, D]), ("g_pre_ffn", [DEPTH, D]), ("g_post_ffn", [DEPTH, D]),
          ("g_ple", [DEPTH, D]), ("rel_bias", [32, 12])]


def bcast_rows(ap1d, n=128):
    a = [list(x) for x in ap1d.ap]
    return bass.AP(tensor=ap1d.tensor, offset=ap1d.offset, ap=[[0, n]] + a)


class Ctx:
    pass


def declare(nc, S, debug):
    T = Ctx()
    T.S = S
    ein = lambda name, shape, dt=F32: nc.dram_tensor(name, list(shape), dt, kind="ExternalInput").ap()
    T.x = ein("x", [S, D])
    T.p = ein("p", [DEPTH, S, PLE])
    T.w = {}
    for name, k, n in WSPECS:
        T.w[name] = ein(name, [DEPTH, k, n])
    T.v = {}
    for name, shape in VSPECS:
        T.v[name] = ein(name, shape)
    T.cmat = ein("cmat", [128, CW])
    T.cs = ein("cs", [S, 128])
    T.y = nc.dram_tensor("y", [S, D], F32, kind="ExternalOutput").ap()
    kind = "ExternalOutput" if debug else "Internal"
    scr = lambda name, shape, dt=BF16: nc.dram_tensor(name, list(shape), dt, kind=kind).ap()
    T.wb = {}
    for name, k, n in WSPECS:
        T.wb[name] = [nc.dram_tensor(f"wb_{name}_{l}", [k, n], BF16, kind="Internal").ap() for l in range(DEPTH)]
    T.xres = scr("xres", [S, D], F32)
    T.z_tm = scr("z_tm", [S, DI])
    T.xbcT = scr("xbcT", [XBC, S])
    T.dt_tm = scr("dt_tm", [S, 64], F32)
    T.da_tm = scr("da_tm", [S, 64], F32)
    T.qT = scr("qT", [QW, S])
    T.kT = scr("kT", [KVW, S])
    T.v_tm = scr("v_tm", [S, KVW])
    T.dqT = scr("dqT", [DW, S])
    T.dkT = scr("dkT", [DW, S])
    T.dv_tm = scr("dv_tm", [S, DW])
    T.gatesT = scr("gatesT", [3 * D, S])
    T.xs_tm = scr("xs_tm", [S, DI])
    T.b_tm = scr("b_tm", [S, GN])
    T.bcT = scr("bcT", [2 * GN, S])
    T.hb = scr("hb", [S // 128, 128, DI])
    T.yssdT = scr("yssdT", [DI, S])
    T.ygqaT = scr("ygqaT", [QW, S])
    T.ydilT = scr("ydilT", [DOUT, S])
    T.ttab = nc.dram_tensor("ttab", [128, 12 * 384], F32, kind="Internal").ap()
    T.debug = debug
    if debug:
        T.dbg_xa = scr("dbg_xa", [512, D], F32)
        T.dbg_xb = scr("dbg_xb", [512, D], F32)
        T.dbg_mo = scr("dbg_mo", [512, D], F32)
        T.dbg_mix = scr("dbg_mix", [D, 512], BF16)
    return T


WDIMS = {name: (k, n) for name, k, n in WSPECS}


def wconv_gen(K, T, st, items, engs, NB=3):
    fin = [K.sb(st, f"wc_f{i}", [128, 2048], F32) for i in range(NB)]
    fo = [K.sb(st, f"wc_b{i}", [128, 2048], BF16) for i in range(NB)]
    work = []
    for name, l in items:
        k, ncols = WDIMS[name]
        for r in range(k // 128):
            for c0 in range(0, ncols, 2048):
                w = min(2048, ncols - c0)
                work.append((T.w[name][l][r * 128:(r + 1) * 128, c0:c0 + w], T.wb[name][l][r * 128:(r + 1) * 128, c0:c0 + w], w))

    def g():
        n = len(work)
        for k in range(n + 2):
            if k - 2 >= 0:
                _, dst, w = work[k - 2]
                K.dma(dst, fo[(k - 2) % NB][:, :w])
            if 0 <= k - 1 < n:
                _, _, w = work[k - 1]
                K.cp(fo[(k - 1) % NB][:, :w], fin[(k - 1) % NB][:, :w], eng=engs[(k - 1) % len(engs)])
            if k < n:
                src, _, w = work[k]
                K.dma(fin[k % NB][:, :w], src)
            yield
    return g()


def phase_wconv(K, T, items, with_ttab=False):
    import contextlib
    with contextlib.ExitStack() as st:
        gen = wconv_gen(K, T, st, items, [K.pool, K.act] if with_ttab else [K.dve, K.pool, K.act])
        if with_ttab:
            cm = K.sb(st, "cm", [128, CW], F32)
            K.dma(cm[:, :], T.cmat[:, :])
            rb = K.sb(st, "rb", [128, 384], F32)
            load_bcast(K, rb[:, :], T.v["rel_bias"].rearrange("a b -> (a b)"))
            Ttab = K.sb(st, "Ttab", [128, 12, 384], F32)
            tmpb = [K.sb(st, f"tmpb{i}", [128, 384], F32) for i in range(2)]
            for hd in range(12):
                g = hd // 4
                BT = cm[:, C_BT + g * 384:C_BT + (g + 1) * 384]
                K.ts(Ttab[:, hd, :], BT, 0.0, NEG, ALU.is_lt, ALU.mult)
                for b in range(32):
                    tb = tmpb[b % 2]
                    K.ts(tb[:, :], BT, float(b), rb[:, b * 12 + hd:b * 12 + hd + 1], ALU.is_equal, ALU.mult)
                    K.tt(Ttab[:, hd, :], Ttab[:, hd, :], tb[:, :], ALU.add)
                    if b % 4 == 3:
                        next(gen, None)
            K.dma(T.ttab.rearrange("p (h c) -> p h c", h=12), Ttab[:, :, :])
        for _ in gen:
            pass
    K.barrier()


def load_bcast(K, tile_tv, vec_ap):
    K.dma(tile_tv, bcast_rows(vec_ap))


def phase_inproj(K, T, l, src_x, bg_items=None):
    import contextlib
    S = T.S
    NG = S // 512
    with contextlib.ExitStack() as st:
        bg = wconv_gen(K, T, st, bg_items, [K.pool]) if bg_items else None
        cm = K.sb(st, "cm", [128, CW], F32)
        K.dma(cm[:, :], T.cmat[:, :])
        identb = K.sb(st, "identb", [128, 128], BF16)
        K.cp(identb[:, :], cm[:, C_ID:C_ID + 128])
        gpre = K.sb(st, "gpre", [128, D], F32)
        load_bcast(K, gpre[:, :], T.v["g_pre_mix"][l])
        gq = K.sb(st, "gq", [128, HD], F32)
        load_bcast(K, gq[:, :], T.v["g_q"][l])
        gk = K.sb(st, "gk", [128, HD], F32)
        load_bcast(K, gk[:, :], T.v["g_k"][l])
        dtb = K.sb(st, "dtb", [128, 64], F32)
        load_bcast(K, dtb[:, :], T.v["dt_bias"][l].rearrange("a b -> (a b)"))
        atab = K.sb(st, "atab", [128, 64], F32)
        load_bcast(K, atab[:, :], T.v["a_log"][l].rearrange("a b -> (a b)"))
        K.actf(atab[:, :], atab[:, :], AF.Exp)
        K.ts(atab[:, :], atab[:, :], -1.0, None, ALU.mult)

        xt = [K.sb(st, f"xt{i}", [128, D], F32) for i in range(2)]
        xn = [K.sb(st, f"xn{i}", [128, D], BF16) for i in range(2)]
        junk = K.sb(st, "junk", [128, D], F32)
        ss = [K.sb(st, f"ss{i}", [128, 1], F32) for i in range(2)]
        psT = [K.ps(st, f"psT{i}", [128, KD, 128], BF16) for i in range(2)]
        hT = [K.sb(st, f"hT{i}", [128, KD, 512], BF16) for i in range(2)]
        NW = 3
        wblk = [K.sb(st, f"wblk{i}", [128, KD, 512], BF16) for i in range(NW)]
        pso = [K.ps(st, f"pso{i}", [128, 512], F32) for i in range(4)]
        psQ = [K.ps(st, f"psQ{i}", [128, 8, 128], BF16) for i in range(2)]
        NSO = 6
        so = [K.sb(st, f"so{i}", [128, 4, 512], BF16) for i in range(NSO)]
        sodt = K.sb(st, "sodt", [128, 4, 64], F32)
        soda = K.sb(st, "soda", [128, 4, 64], F32)
        tmpdt = K.sb(st, "tmpdt", [128, 64], F32)
        cst = [K.sb(st, f"cst{i}", [128, 4, 128], F32) for i in range(2)]
        sq = [K.sb(st, f"sq{i}", [128, 512], F32) for i in range(2)]
        qss = [K.sb(st, f"qss{i}", [128, 4], F32) for i in range(4)]
        qf = [K.sb(st, f"qf{i}", [128, 512], F32) for i in range(4)]
        rt = [[K.sb(st, f"rt{j}_{i}", [128, 4, 64], F32) for i in range(4)] for j in range(2)]
        qr = [K.sb(st, f"qr{i}", [128, 512], BF16) for i in range(4)]
        deferred = []

        def blk(kind, base, dst, n):
            return [(base + 512 * b_, 512, kind, dst, 512 * b_) for b_ in range(n)]

        zb = blk("tm", O_Z, T.z_tm, 4)
        xb = blk("fm", O_XBC, T.xbcT, 6)
        qb = blk("qk", O_GQ, T.qT, 4)
        kb = [(O_GK, 512, "qk", T.kT, 0)]
        vb = [(O_GV, 512, "tm", T.v_tm, 0)]
        dqb = blk("fm", O_DQ, T.dqT, 3)
        dkb = blk("fm", O_DK, T.dkT, 3)
        dvb = blk("tm", O_DV, T.dv_tm, 3)
        gb = blk("fm", O_GT, T.gatesT, 6)
        dtb_ = [(O_DT, 64, "dt", None, 0)]
        others = zb + xb + dtb_ + vb + dqb + dkb + dvb + gb
        qks = qb + kb
        blocks = []
        per = len(others) // len(qks)
        oi = 0
        for qi, qk_ in enumerate(qks):
            blocks.append(qk_)
            take = per if qi < len(qks) - 1 else len(others) - oi
            blocks += others[oi:oi + take]
            oi += take
        NB = len(blocks)
        wsrc = T.wb["w_in"][l].rearrange("(kc p) n -> p kc n", p=128)

        seq = [(tg, bi) for tg in range(NG) for bi in range(NB)]

        def load_w(idx):
            tg, bi = seq[idx]
            c0, w = blocks[bi][0], blocks[bi][1]
            K.dma(wblk[idx % NW][:, :, :w], wsrc[:, :, c0:c0 + w])

        cnt = {"pso": 0, "so": 0, "ev": 0, "q": 0}

        def evac(out, in_):
            cnt["ev"] += 1
            if cnt["ev"] % 2:
                K.cp(out, in_, eng=K.act)
            else:
                K.cp(out, in_, eng=K.dve)

        load_w(0)
        load_w(1)
        for tg in range(NG):
            t0 = tg * 512
            h = hT[tg % 2]
            c_t = cst[tg % 2]
            K.dma(c_t[:, :, :], T.cs[t0:t0 + 512, :].rearrange("(i p) c -> p i c", p=128))
            for i in range(4):
                xi = xt[i % 2]
                K.dma(xi[:, :], src_x[t0 + i * 128:t0 + (i + 1) * 128, :])
                s_ = ss[i % 2]
                K.actf(junk[:, :], xi[:, :], AF.Square, accum=s_[:, :])
                K.ts(s_[:, :], s_[:, :], 1.0 / D, EPS, ALU.mult, ALU.add)
                K.actf(s_[:, :], s_[:, :], AF.Sqrt)
                K.recip(s_[:, :], s_[:, :])
                K.stt(xn[i % 2][:, :], xi[:, :], s_[:, :], gpre[:, :], ALU.mult, ALU.mult)
                pt = psT[i % 2]
                for kc in range(KD):
                    K.tr(pt[:, kc, :], xn[i % 2][:, kc * 128:(kc + 1) * 128], identb[:, :], sig=(kc == KD - 1))
                evac(h[:, :, i * 128:(i + 1) * 128], pt[:, :, :])
            warm(K, pso[cnt["pso"] % 4][:, :], identb[:, :], h[:, 0, :], n=12)
            for bi in range(NB):
                idx = tg * NB + bi
                if idx + 2 < len(seq):
                    load_w(idx + 2)
                wb_ = wblk[idx % NW]
                c0, w, kind, dst, doff = blocks[bi]
                if bg is not None:
                    next(bg, None)
                if kind == "tm":
                    s_o = so[cnt["so"] % NSO]
                    cnt["so"] += 1
                    for i in range(4):
                        po = pso[cnt["pso"] % 4]
                        cnt["pso"] += 1
                        for kc in range(KD):
                            K.mm(po[:, :], h[:, kc, i * 128:(i + 1) * 128], wb_[:, kc, :], start=(kc == 0),
                                 stop=(kc == KD - 1), sig=(kc == KD - 1))
                        evac(s_o[:, i, :], po[:, :])
                    K.dma(dst[t0:t0 + 512, doff:doff + 512].rearrange("(i p) n -> p i n", p=128), s_o[:, :, :])
                elif kind == "fm":
                    s_o = so[cnt["so"] % NSO]
                    cnt["so"] += 1
                    for j in range(4):
                        po = pso[cnt["pso"] % 4]
                        cnt["pso"] += 1
                        for kc in range(KD):
                            K.mm(po[:, :], wb_[:, kc, j * 128:(j + 1) * 128], h[:, kc, :], start=(kc == 0),
                                 stop=(kc == KD - 1), sig=(kc == KD - 1))
                        evac(s_o[:, j, :], po[:, :])
                    K.dma(dst[doff:doff + 512, t0:t0 + 512].rearrange("(j p) s -> p j s", p=128), s_o[:, :, :])
                elif kind == "dt":
                    for i in range(4):
                        po = pso[cnt["pso"] % 4]
                        cnt["pso"] += 1
                        for kc in range(KD):
                            K.mm(po[:, :64], h[:, kc, i * 128:(i + 1) * 128], wb_[:, kc, :64], start=(kc == 0),
                                 stop=(kc == KD - 1), sig=(kc == KD - 1))
                        K.tt(tmpdt[:, :], po[:, :64], dtb[:, :], ALU.add)
                        K.actf(tmpdt[:, :], tmpdt[:, :], AF.Exp)
                        K.actf(sodt[:, i, :], tmpdt[:, :], AF.Ln, bias=1.0)
                        K.tt(soda[:, i, :], sodt[:, i, :], atab[:, :], ALU.mult)
                    K.dma(T.dt_tm[t0:t0 + 512, :].rearrange("(i p) n -> p i n", p=128), sodt[:, :, :])
                    K.dma(T.da_tm[t0:t0 + 512, :].rearrange("(i p) n -> p i n", p=128), soda[:, :, :])
                elif kind == "qk":
                    gtab = gk if dst is T.kT else gq
                    s_o = so[cnt["so"] % NSO]
                    cnt["so"] += 1
                    qrs = []
                    for i in range(4):
                        po = pso[cnt["pso"] % 4]
                        cnt["pso"] += 1
                        for kc in range(KD):
                            K.mm(po[:, :], h[:, kc, i * 128:(i + 1) * 128], wb_[:, kc, :], start=(kc == 0),
                                 stop=(kc == KD - 1), sig=(kc == KD - 1))
                        qi = cnt["q"] % 4
                        cnt["q"] += 1
                        K.cp(qf[qi][:, :], po[:, :], eng=K.act)
                        K.actf(sq[qi % 2][:, :], qf[qi][:, :], AF.Square)
                        K.rsum(qss[qi][:, :], sq[qi % 2][:, :].rr("p (h d) -> p h d", h=4))
                        K.ts(qss[qi][:, :], qss[qi][:, :], 1.0 / HD, EPS, ALU.mult, ALU.add)
                        K.actf(qss[qi][:, :], qss[qi][:, :], AF.Sqrt)
                        K.recip(qss[qi][:, :], qss[qi][:, :])
                        q3 = qf[qi][:, :].rr("p (h d) -> p h d", h=4)
                        K.tt(q3, q3, qss[qi][:, :].rr("p (h o) -> p h o", o=1).bc(2, HD), ALU.mult)
                        K.tt(q3, q3, gtab[:, :].rr("p (o d) -> p o d", o=1).bc(1, 4), ALU.mult)
                        q4 = qf[qi][:, :].rr("p (h d two) -> p h d two", h=4, two=2)
                        x0 = TV(q4.tile, q4.ap[:, :, :, 0])
                        x1 = TV(q4.tile, q4.ap[:, :, :, 1])
                        cc = c_t[:, i, 0:64].rr("p (o d) -> p o d", o=1).bc(1, 4)
                        sn = c_t[:, i, 64:128].rr("p (o d) -> p o d", o=1).bc(1, 4)
                        o4 = qr[qi][:, :].rr("p (h d two) -> p h d two", h=4, two=2)
                        o0 = TV(o4.tile, o4.ap[:, :, :, 0])
                        o1 = TV(o4.tile, o4.ap[:, :, :, 1])
                        r_ = rt[qi % 2]
                        K.tt(r_[0][:, :, :], x0, cc, ALU.mult)
                        K.tt(r_[1][:, :, :], x1, sn, ALU.mult)
                        K.tt(o0, r_[0][:, :, :], r_[1][:, :, :], ALU.subtract)
                        K.tt(r_[2][:, :, :], x0, sn, ALU.mult, eng=K.pool)
                        K.tt(r_[3][:, :, :], x1, cc, ALU.mult, eng=K.pool)
                        K.tt(o1, r_[2][:, :, :], r_[3][:, :, :], ALU.add, eng=K.pool)
                        qrs.append(qr[qi])

                    def fin(qrs=qrs, s_o=s_o, dst=dst, doff=doff, t0=t0):
                        for i, qr_ in enumerate(qrs):
                            pq = psQ[i % 2]
                            for hh in range(4):
                                K.tr(pq[:, hh, :], qr_[:, hh * 128:(hh + 1) * 128], identb[:, :], sig=(hh == 3))
                            evac(s_o[:, :, i * 128:(i + 1) * 128], pq[:, 0:4, :])
                        K.dma(dst[doff:doff + 512, t0:t0 + 512].rearrange("(h d) s -> d h s", d=128), s_o[:, :, :])

                    deferred.append([3, fin])
                if kind != "qk":
                    for dfr in deferred:
                        dfr[0] -= 1
                    while deferred and deferred[0][0] <= 0:
                        deferred.pop(0)[1]()
            while deferred:
                deferred.pop(0)[1]()
        if bg is not None:
            for _ in bg:
                pass
    K.barrier()


def build(S, debug=False, phases=None):
    nc = bass.Bass("TRN2", target_bir_lowering=False)
    T = declare(nc, S, debug)
    K = Kern(nc)
    ph = phases or ["all"]
    allp = "all" in ph
    allw = [(name, l) for l in range(DEPTH) for name, _, _ in WSPECS]
    if allp:
        phase_wconv(K, T, [("w_in", 0)], with_ttab=True)
        bg_items = [it for it in allw if it != ("w_in", 0)]
        bg_items1 = None
    else:
        bg_items = None
        bg_items1 = None
        if "wconv" in ph:
            phase_wconv(K, T, allw, with_ttab=True)
    for l in range(DEPTH):
        src_x = T.x if l == 0 else T.xres
        dst_x = T.xres if l < DEPTH - 1 else T.y
        if allp or "inproj" in ph:
            phase_inproj(K, T, l, src_x, None)
        if allp or "ssd" in ph:
            phase_ssd(K, T, l, bg_items1 if l == 0 else None)
        if allp or "gqa" in ph:
            phase_gqa(K, T, l, bg_items if l == 0 else None)
        if allp or "dil" in ph:
            phase_dil(K, T, l)
        if allp or "mix" in ph:
            phase_mix(K, T, l, src_x, dst_x)
        if not allp and "onelayer" in ph:
            break
    K.barrier()
    return nc, K


def core_inputs(S, x, p, W, consts):
    cmat, cs = consts
    m = {"x": np.ascontiguousarray(x, dtype=np.float32), "p": np.ascontiguousarray(p, dtype=np.float32),
         "cmat": cmat, "cs": cs}
    for name, _, _ in WSPECS:
        m[name] = W[name]
    for name, _ in VSPECS:
        m[name] = W[name]
    return m


def _v3(tv, a, b):
    return tv.rr("p (a b) -> p a b", a=a, b=b)


def _bl(tv, n):
    return tv.rr("p (a o) -> p a o", o=1).bc(2, n)


def _bm(tv, n):
    return tv.rr("p (o b) -> p o b", o=1).bc(1, n)


def phase_ssd(K, T, l, bg_items=None):
    import contextlib
    S = T.S
    NG = S // 512
    NT = S // 128
    with contextlib.ExitStack() as st:
        cm = K.sb(st, "cm", [128, CW], F32)
        K.dma(cm[:, :], T.cmat[:, :])
        identf = cm[:, C_ID:C_ID + 128]
        identb = K.sb(st, "identb", [128, 128], BF16)
        K.cp(identb[:, :], identf)
        cwb = K.sb(st, "cwb", [6, XBC], F32)
        K.dma(cwb[0:5, :], T.v["conv_w"][l])
        K.dma(cwb[5:6, :], T.v["conv_b"][l].rearrange("(o n) -> o n", o=1))
        pcw = K.ps(st, "pcw", [128, 64, 8], F32)
        for j in range(24):
            K.tr(pcw[:, j, 0:6], cwb[0:6, j * 128:(j + 1) * 128], cm[0:6, C_ID:C_ID + 6], sig=(j == 23))
        cwT = K.sb(st, "cwT", [128, 24, 8], F32)
        K.cp(cwT[:, :, 0:6], pcw[:, 0:24, 0:6])
        dg = K.sb(st, "dg", [128, 24 * 5, 128], BF16)
        for j in range(24):
            for k in range(5):
                K.ts(dg[:, j * 5 + k, :], identf, cwT[:, j, k:k + 1], None, ALU.mult,
                     eng=(K.dve if (j * 5 + k) % 2 else K.pool))
        xin = [K.sb(st, f"xin{i}", [128, 24, 516], BF16) for i in range(2)]
        xo = [K.sb(st, f"xo{i}", [128, 24, 512], BF16) for i in range(2)]
        pso = [K.ps(st, f"pc{i}", [128, 512], F32) for i in range(3)]
        pT = [K.ps(st, f"pT{i}", [128, 8, 128], BF16) for i in range(2)]
        xst = K.sb(st, "xst", [128, 4, DI], BF16)
        bst = K.sb(st, "bst", [128, 4, GN], BF16)
        srcv = T.xbcT.rearrange("(j p) s -> p j s", p=128)
        n_pt = 0
        for tg in range(NG):
            t0 = tg * 512
            xi = xin[tg % 2]
            lo = 2 if tg == 0 else 0
            hi = 514 if tg == NG - 1 else 516
            if tg == 0:
                K.mset(xi[:, :, 0:2], 0.0)
            if tg == NG - 1:
                K.mset(xi[:, :, 514:516], 0.0)
            K.dma(xi[:, :, lo:hi], srcv[:, :, t0 - 2 + lo:t0 - 2 + hi])
            xo_ = xo[tg % 2]
            for j in range(24):
                pc = pso[j % 3]
                for k in range(5):
                    K.mm(pc[:, :], dg[:, j * 5 + k, :], xi[:, j, k:k + 512], start=(k == 0), stop=(k == 4), sig=(k == 4))
                K.actf(xo_[:, j, :], pc[:, :], AF.Silu, bias=cwT[:, j, 5:6])
            K.dma(T.bcT[:, t0:t0 + 512].rearrange("(j p) s -> p j s", p=128), xo_[:, 16:24, :])
            for i in range(4):
                for rnd in range(3):
                    pt = pT[n_pt % 2]
                    n_pt += 1
                    nj = 8 if rnd < 2 else 4
                    for jj in range(nj):
                        j = rnd * 8 + jj
                        K.tr(pt[:, jj, :], xo_[:, j, i * 128:(i + 1) * 128], identb[:, :], sig=(jj == nj - 1))
                    if rnd < 2:
                        K.cp(_v3(xst[:, i, rnd * 1024:(rnd + 1) * 1024], 8, 128), pt[:, :, :],
                             eng=(K.dve if rnd else K.act))
                    else:
                        K.cp(_v3(bst[:, i, :], 4, 128), pt[:, 0:4, :], eng=K.dve)
            K.dma(T.xs_tm[t0:t0 + 512, :].rearrange("(i p) n -> p i n", p=128), xst[:, :, :])
            K.dma(T.b_tm[t0:t0 + 512, :].rearrange("(i p) n -> p i n", p=128), bst[:, :, :])
    K.barrier()

    with contextlib.ExitStack() as st:
        cm = K.sb(st, "cm", [128, CW], F32)
        K.dma(cm[:, :], T.cmat[:, :])
        xs_c = [K.sb(st, f"xs{i}", [128, DI], BF16) for i in range(2)]
        b_c = [K.sb(st, f"bc{i}", [128, GN], BF16) for i in range(2)]
        dtda = [K.sb(st, f"dtda{i}", [128, 128], F32) for i in range(2)]
        pss = K.ps(st, "pss", [128, 512], F32)
        ew = [K.sb(st, f"ew{i}", [128, 64], F32) for i in range(2)]
        wsc = [K.sb(st, f"wsc{i}", [128, 32], F32) for i in range(2)]
        xcs = [K.sb(st, f"xcs{i}", [128, DI], BF16) for i in range(2)]
        pst = K.ps(st, "pst", [128, DI], F32)
        H = K.sb(st, "H", [128, DI], F32)
        Hbf = [K.sb(st, f"Hbf{i}", [128, DI], BF16) for i in range(2)]
        K.mset(H[:, :], 0.0)
        bg = wconv_gen(K, T, st, bg_items, [K.pool, K.act]) if bg_items else None
        nbg = (sum((WDIMS[nm][0] // 128) * ((WDIMS[nm][1] + 2047) // 2048) for nm, _ in bg_items) + NT - 1) // NT if bg_items else 0
        for n, c in enumerate(range(NT - 1, -1, -1)):
            r0 = c * 128
            a = n % 2
            for _ in range(nbg):
                next(bg, None)
            K.dma(xs_c[a][:, :], T.xs_tm[r0:r0 + 128, :])
            K.dma(b_c[a][:, :], T.b_tm[r0:r0 + 128, :])
            K.dma(dtda[a][:, 0:64], T.dt_tm[r0:r0 + 128, :])
            K.dma(dtda[a][:, 64:128], T.da_tm[r0:r0 + 128, :])
            da_b = dtda[a][:, 96:128]
            dt_b = dtda[a][:, 32:64]
            K.mm(pss[:, 0:32], cm[:, C_US:C_US + 128], da_b, sig=False)
            K.mm(pss[:, 32:64], cm[:, C_ONE:C_ONE + 128], da_b)
            K.actf(ew[a][:, :], pss[:, 0:64], AF.Exp)
            K.tt(wsc[a][:, :], ew[a][:, 0:32], dt_b, ALU.mult)
            K.tt(_v3(xcs[a][:, :], NH, HP), _v3(xs_c[a][:, :], NH, HP), _bl(wsc[a][:, :], HP), ALU.mult)
            K.cp(Hbf[a][:, :], H[:, :], eng=K.act)
            K.dma(T.hb[c], Hbf[a][:, :])
            for g in range(4):
                K.mm(pst[:, g * 512:(g + 1) * 512], b_c[a][:, g * 128:(g + 1) * 128], xcs[a][:, g * 512:(g + 1) * 512],
                     sig=(g == 3))
            K.tt(_v3(H[:, :], NH, HP), _v3(H[:, :], NH, HP), _bl(ew[a][:, 32:64], HP), ALU.mult)
            K.tt(H[:, :], H[:, :], pst[:, :], ALU.add)
        if bg is not None:
            for _ in bg:
                pass
    K.barrier()

    with contextlib.ExitStack() as st:
        cm = K.sb(st, "cm", [128, CW], F32)
        K.dma(cm[:, :], T.cmat[:, :])
        identb = K.sb(st, "identb", [128, 128], BF16)
        K.cp(identb[:, :], cm[:, C_ID:C_ID + 128])
        gssd = K.sb(st, "gssd", [128, DI], F32)
        load_bcast(K, gssd[:, :], T.v["g_ssd"][l])
        dsk = K.sb(st, "dsk", [128, NH], F32)
        load_bcast(K, dsk[:, :], T.v["d_skip"][l])
        two = lambda name, shape, dt: [K.sb(st, f"{name}{i}", shape, dt) for i in range(2)]
        xs_c = two("xs", [128, DI], BF16)
        b_c = two("bc", [128, GN], BF16)
        bcT_c = two("bcT", [128, 8, 128], BF16)
        dtda = two("dtda", [128, 128], F32)
        z_c = two("z", [128, DI], BF16)
        hb_c = two("hb", [128, DI], BF16)
        ew = two("ew", [128, 256], F32)
        wf = two("wf", [128, 32], F32)
        xc_f = two("xc_f", [128, DI], BF16)
        xc_b = two("xc_b", [128, DI], BF16)
        xcs_f = two("xcs_f", [128, DI], BF16)
        cbf = two("cbf", [128, 512], F32)
        cbb = two("cbb", [128, 512], F32)
        Rg = [two("Rf", [128, 8 * 128], F32), two("Rb", [128, 8 * 128], F32)]
        Hfb = two("Hfb", [128, DI], BF16)
        pss = K.ps(st, "pss", [128, 512], F32)
        psh = K.ps(st, "psh", [128, 512], F32)
        pseg = [K.ps(st, f"pseg{i}", [128, 512], F32) for i in range(2)]
        py = K.ps(st, "py", [128, DI], F32)
        ebuf = [K.sb(st, f"ebuf{i}", [128, 512], F32) for i in range(3)]
        mT = [K.sb(st, f"mT{i}", [128, 512], BF16) for i in range(3)]
        yacc = K.sb(st, "yacc", [128, DI], F32)
        ytmp = K.sb(st, "ytmp", [128, DI], F32)
        sz = K.sb(st, "sz", [128, DI], F32)
        ssq = K.sb(st, "ssq", [128, 1], F32)
        yn = K.sb(st, "yn", [128, DI], BF16)
        Hf = K.sb(st, "Hf", [128, DI], F32)
        yTs = K.sb(st, "yTs", [128, 16, 512], BF16)
        K.mset(Hf[:, :], 0.0)
        K.mset(Hfb[0][:, :], 0.0)
        bcv = T.bcT.rearrange("(j p) s -> p j s", p=128)
        rot = [psh, pseg[0], pseg[1]]
        cnt = {"r": 0}

        def nrot():
            cnt["r"] += 1
            return rot[cnt["r"] % 3]

        def loads(c):
            r0 = c * 128
            a = c % 2
            K.dma(xs_c[a][:, :], T.xs_tm[r0:r0 + 128, :])
            K.dma(b_c[a][:, :], T.b_tm[r0:r0 + 128, :])
            K.dma(bcT_c[a][:, :, :], bcv[:, :, r0:r0 + 128])
            K.dma(dtda[a][:, 0:64], T.dt_tm[r0:r0 + 128, :])
            K.dma(dtda[a][:, 64:128], T.da_tm[r0:r0 + 128, :])
            K.dma(z_c[a][:, :], T.z_tm[r0:r0 + 128, :])
            K.dma(hb_c[a][:, :], T.hb[c])

        def front(c):
            a = c % 2
            da = dtda[a][:, 64:128]
            K.mm(pss[:, 0:64], cm[:, C_U:C_U + 128], da, sig=False)
            K.mm(pss[:, 64:128], cm[:, C_LS:C_LS + 128], da, sig=False)
            K.mm(pss[:, 128:192], cm[:, C_L:C_L + 128], da, sig=False)
            K.mm(pss[:, 192:256], cm[:, C_ONE:C_ONE + 128], da)
            K.actf(ew[a][:, :], pss[:, 0:256], AF.Exp)
            K.tt(wf[a][:, :], ew[a][:, 64:96], dtda[a][:, 0:32], ALU.mult)
            xs3 = _v3(xs_c[a][:, :], NH, HP)
            K.tt(_v3(xc_f[a][:, :], NH, HP), xs3, _bl(dtda[a][:, 0:32], HP), ALU.mult)
            K.tt(_v3(xc_b[a][:, :], NH, HP), xs3, _bl(dtda[a][:, 32:64], HP), ALU.mult, eng=K.pool)
            K.tt(_v3(xcs_f[a][:, :], NH, HP), xs3, _bl(wf[a][:, :], HP), ALU.mult, eng=K.pool)
            for g in range(4):
                K.mm(psh[:, g * 128:(g + 1) * 128], bcT_c[a][:, g, :], bcT_c[a][:, 4 + g, :], sig=(g == 3))
            K.tt(_v3(cbf[a][:, :], 4, 128), _v3(psh[:, :], 4, 128), _bm(cm[:, C_U:C_U + 128], 4), ALU.mult)
            K.tt(_v3(cbb[a][:, :], 4, 128), _v3(psh[:, :], 4, 128), _bm(cm[:, C_L:C_L + 128], 4), ALU.mult)

        def rgen(c, g):
            a = c % 2
            K.tt(_v3(Rg[0][g % 2][:, :], 8, 128), _bm(cm[:, C_U:C_U + 128], 8),
                 _bl(dtda[a][:, 64 + 8 * g:64 + 8 * g + 8], 128), ALU.mult)
            K.tt(_v3(Rg[1][g % 2][:, :], 8, 128), _bm(cm[:, C_L:C_L + 128], 8),
                 _bl(dtda[a][:, 96 + 8 * g:96 + 8 * g + 8], 128), ALU.mult)

        quads = [(g, dr, q) for g in range(4) for dr in range(2) for q in range(2)]

        def seg_stage(c, k):
            a = c % 2
            g, dr, q = quads[k]
            R = Rg[dr][g % 2]
            lmask = C_LS if dr == 0 else C_US
            cb = cbf[a] if dr == 0 else cbb[a]
            ps_ = pseg[k % 2]
            eb = ebuf[k % 3]
            m_ = mT[k % 3]
            K.mm(ps_[:, :], cm[:, lmask:lmask + 128], R[:, q * 512:(q + 1) * 512])
            K.actf(eb[:, :], ps_[:, :], AF.Exp)
            K.tt(_v3(m_[:, :], 4, 128), _v3(eb[:, :], 4, 128), _bm(cb[:, g * 128:(g + 1) * 128], 4), ALU.mult,
                 eng=(K.dve if k % 3 else K.pool))

        def y_stage(c, k):
            a = c % 2
            g, dr, q = quads[k]
            xc = xc_f[a] if dr == 0 else xc_b[a]
            h0 = g * 8 + q * 4
            m_ = mT[k % 3]
            for hh in range(4):
                h = h0 + hh
                first = (dr == 0 and q == 0 and hh == 0)
                last = (dr == 1 and q == 1 and hh == 3)
                K.mm(py[:, h * HP:(h + 1) * HP], m_[:, hh * 128:(hh + 1) * 128], xc[:, h * HP:(h + 1) * HP],
                     start=first, stop=last, sig=(hh == 3))

        loads(0)
        if NT > 1:
            loads(1)
        front(0)
        for c in range(NT):
            a = c % 2
            xs3 = _v3(xs_c[a][:, :], NH, HP)
            for g in range(4):
                gs = slice(g * 512, (g + 1) * 512)
                p1 = nrot()
                K.mm(p1[:, :], bcT_c[a][:, 4 + g, :], Hfb[a][:, gs])
                K.tt(_v3(yacc[:, gs], 8, HP), _v3(p1[:, :], 8, HP), _bl(ew[a][:, 8 * g:8 * g + 8], HP), ALU.mult)
                p2 = nrot()
                K.mm(p2[:, :], bcT_c[a][:, 4 + g, :], hb_c[a][:, gs])
                K.tt(_v3(ytmp[:, gs], 8, HP), _v3(p2[:, :], 8, HP), _bl(ew[a][:, 160 + 8 * g:160 + 8 * g + 8], HP), ALU.mult)
                K.tt(yacc[:, gs], yacc[:, gs], ytmp[:, gs], ALU.add, eng=K.pool)
            K.tt(_v3(Hf[:, :], NH, HP), _v3(Hf[:, :], NH, HP), _bl(ew[a][:, 192:224], HP), ALU.mult)
            for g in range(4):
                gs = slice(g * 512, (g + 1) * 512)
                p1 = nrot()
                K.mm(p1[:, :], b_c[a][:, g * 128:(g + 1) * 128], xcs_f[a][:, gs])
                K.tt(Hf[:, gs], Hf[:, gs], p1[:, :], ALU.add)
            K.cp(Hfb[(c + 1) % 2][:, :], Hf[:, :], eng=K.act)
            rgen(c, 0)
            seg_stage(c, 0)
            for k in range(16):
                if k % 4 == 0 and k // 4 + 1 < 4:
                    rgen(c, k // 4 + 1)
                if k + 1 < 16:
                    seg_stage(c, k + 1)
                y_stage(c, k)
                if k == 6 and c + 1 < NT:
                    front(c + 1)
            K.tt(yacc[:, :], yacc[:, :], py[:, :], ALU.add)
            K.tt(_v3(ytmp[:, :], NH, HP), xs3, _bl(dsk[:, :], HP), ALU.mult, eng=K.pool)
            K.tt(yacc[:, :], yacc[:, :], ytmp[:, :], ALU.add)
            K.actf(sz[:, :], z_c[a][:, :], AF.Silu)
            K.tt(yacc[:, :], yacc[:, :], sz[:, :], ALU.mult)
            K.actf(sz[:, :], yacc[:, :], AF.Square, accum=ssq[:, :])
            K.ts(ssq[:, :], ssq[:, :], 1.0 / DI, EPS, ALU.mult, ALU.add)
            K.actf(ssq[:, :], ssq[:, :], AF.Sqrt)
            K.recip(ssq[:, :], ssq[:, :])
            K.stt(yn[:, :], yacc[:, :], ssq[:, :], gssd[:, :], ALU.mult, ALU.mult)
            if c + 2 < NT:
                loads(c + 2)
            ci = c % 4
            for rnd in range(2):
                ptv = TV(pseg[rnd], pseg[rnd].h[:, :].bitcast(BF16)).rr("p (j t) -> p j t", j=8)
                for jj in range(8):
                    j = rnd * 8 + jj
                    K.tr(TV(ptv.tile, ptv.ap[:, jj, :]), yn[:, j * 128:(j + 1) * 128], identb[:, :], sig=(jj == 7))
                K.cp(yTs[:, rnd * 8:(rnd + 1) * 8, ci * 128:(ci + 1) * 128], ptv, eng=K.act)
            if ci == 3:
                K.dma(T.yssdT[:, (c - 3) * 128:(c + 1) * 128].rearrange("(j p) s -> p j s", p=128), yTs[:, :, :])
    K.barrier()


def phase_gqa(K, T, l, bg_items=None):
    import contextlib
    S = T.S
    NG = S // 512
    NT = S // 128
    NP = NT // 2
    with contextlib.ExitStack() as st:
        bg = wconv_gen(K, T, st, bg_items, [K.dve], NB=4) if bg_items else None
        onesf = K.sb(st, "onesf", [128, 128], F32)
        K.mset(onesf[:, :], 1.0)
        ones = K.sb(st, "ones", [128, 128], BF16)
        K.cp(ones[:, :], onesf[:, :])
        kT = [K.sb(st, f"kT{i}", [128, S], BF16) for i in range(2)]
        vg = [K.sb(st, f"vg{i}", [128, NT, 128], BF16) for i in range(2)]
        qT = [K.sb(st, f"qT{i}", [128, S], BF16) for i in range(2)]
        pS = [K.ps(st, f"pS{i}", [128, 1024], F32) for i in range(3)]
        pO = [K.ps(st, f"pO{i}", [128, 512], F32) for i in range(1)]
        pD = [K.ps(st, f"pD{i}", [128, 512], F32) for i in range(1)]
        NPT = 6
        pt = [K.sb(st, f"pt{i}", [128, 1024], BF16) for i in range(NPT)]
        psm = [K.sb(st, f"psm{i}", [128, 512], BF16) for i in range(6)]
        NQS = 6
        qsm = [K.sb(st, f"qsm{i}", [128, 512], BF16) for i in range(NQS)]
        rinv = [K.sb(st, f"rinv{i}", [128, 512], F32) for i in range(2)]
        oT = [K.sb(st, f"oT{i}", [128, 512], BF16) for i in range(2)]
        vv = T.v_tm.rearrange("(j p) d -> p j d", p=128)

        def load_kv(g):
            K.dma(kT[g % 2][:, :], T.kT[g * 128:(g + 1) * 128, :])
            step = 16
            for j0 in range(0, NT, step):
                K.dma(vg[g % 2][:, j0:j0 + step, :], vv[:, j0:j0 + step, g * 128:(g + 1) * 128])

        def load_q(h):
            K.dma(qT[h % 2][:, :], T.qT[h * 128:(h + 1) * 128, :])

        load_kv(0)
        load_q(0)
        n = 0
        it = 0
        nq = 0
        fin_state = {"f": None}
        for g in range(4):
            if g + 1 < 4:
                load_kv(g + 1)
            k_, v_ = kT[g % 2], vg[g % 2]
            for r in range(4):
                h = g * 4 + r
                if h + 1 < 16:
                    load_q(h + 1)
                q_ = qT[h % 2]
                warm(K, pS[n % 3][:, 0:512], ones[:, :], q_[:, 0:512], n=(16 if r == 0 else 10))
                for qg in range(NG):
                    if bg is not None:
                        next(bg, None)
                    po, pd = pO[0], pD[0]
                    qs = q_[:, qg * 512:(qg + 1) * 512]
                    slot = {}
                    pend_den = []
                    nden = NP // 2
                    for p in range(NP + 2):
                        if p < NP:
                            cur = n % NPT
                            ps_ = pS[n % 3]
                            n += 1
                            slot[p] = cur
                            K.mm(ps_[:, 0:512], k_[:, (2 * p) * 128:(2 * p + 1) * 128], qs)
                            K.mm(ps_[:, 512:1024], k_[:, (2 * p + 1) * 128:(2 * p + 2) * 128], qs)
                            K.actf(pt[cur][:, :], ps_[:, :], AF.Exp, scale=ATT_SCALE)
                        while pend_den and pend_den[0][2] <= p:
                            m_, t_, _ = pend_den.pop(0)
                            K.mm(pd[:, :], ones[:, :], t_[:, :], start=(m_ == 0), stop=(m_ == nden - 1))
                        pp = p - 2
                        if pp == 0 and fin_state["f"] is not None:
                            fin_state["f"]()
                            fin_state["f"] = None
                        if 0 <= pp < NP:
                            c_ = slot[pp]
                            K.mm(po[:, :], v_[:, 2 * pp, :], pt[c_][:, 0:512], start=(pp == 0), stop=False, sig=False)
                            K.mm(po[:, :], v_[:, 2 * pp + 1, :], pt[c_][:, 512:1024], start=False, stop=(pp == NP - 1))
                            eng = K.dve if pp % 2 == 0 else K.pool
                            K.tt(psm[pp % 6][:, :], pt[c_][:, 0:512], pt[c_][:, 512:1024], ALU.add, eng=eng)
                            if pp % 2 == 1:
                                m_ = pp // 2
                                t_ = qsm[nq % NQS]
                                nq += 1
                                K.tt(t_[:, :], psm[(pp - 1) % 6][:, :], psm[pp % 6][:, :], ALU.add, eng=K.dve)
                                pend_den.append((m_, t_, p + 4))
                    def finish(pend_den=pend_den, it=it, h=h, qg=qg, po=po, pd=pd, nden=nden):
                        for m_, t_, _ in pend_den:
                            K.mm(pd[:, :], ones[:, :], t_[:, :], start=(m_ == 0), stop=(m_ == nden - 1))
                        K.recip(rinv[it % 2][:, :], pd[:, :])
                        K.tt(oT[it % 2][:, :], po[:, :], rinv[it % 2][:, :], ALU.mult)
                        K.dma(T.ygqaT[h * 128:(h + 1) * 128, qg * 512:(qg + 1) * 512], oT[it % 2][:, :])

                    fin_state["f"] = finish
                    it += 1
        if fin_state["f"] is not None:
            fin_state["f"]()
        if bg is not None:
            for _ in bg:
                pass
    K.barrier()


def phase_dil(K, T, l):
    import contextlib
    S = T.S
    with contextlib.ExitStack() as st:
        cm = K.sb(st, "cm", [128, CW], F32)
        K.dma(cm[:, :], T.cmat[:, :])
        onesf = K.sb(st, "onesf", [128, 128], F32)
        K.mset(onesf[:, :], 1.0)
        ones = K.sb(st, "ones", [128, 128], BF16)
        K.cp(ones[:, :], onesf[:, :])
        rb = K.sb(st, "rb", [128, 384], F32)
        load_bcast(K, rb[:, :], T.v["rel_bias"].rearrange("a b -> (a b)"))
        Ttab = K.sb(st, "Ttab", [128, 12, 384], F32)
        tmpb = [K.sb(st, f"tmpb{i}", [128, 384], F32) for i in range(2)]
        if False:
            pass
        else:
            K.dma(Ttab[:, :, :], T.ttab.rearrange("p (h c) -> p h c", h=12))
        qT = K.sb(st, "qT", [128, S], BF16)
        kT = K.sb(st, "kT", [128, S], BF16)
        vrs = [K.sb(st, f"vr{i}", [128, S // 128, 128], BF16) for i in range(2)]
        num = K.sb(st, "num", [128, S], F32)
        den = K.sb(st, "den", [128, S], F32)
        pS = [K.ps(st, f"pS{i}", [128, 512], F32) for i in range(2)]
        qTr = K.sb(st, "qTr", [128, S], BF16)
        kTr = K.sb(st, "kTr", [128, S], BF16)
        pO = [K.ps(st, f"pO{i}", [128, 512], F32) for i in range(2)]
        pD = [K.ps(st, f"pD{i}", [128, 512], F32) for i in range(2)]
        sb_ = [K.sb(st, f"sb{i}", [128, 384], F32) for i in range(3)]
        pt = [K.sb(st, f"pt{i}", [128, 384], BF16) for i in range(3)]
        ost = [K.sb(st, f"ost{i}", [128, 2048], BF16) for i in range(2)]
        n = 0
        nb_ = 0
        heads = [(j, g) for j in range(4) for g in range(3)]

        def load_head(idx):
            j_, g_ = heads[idx]
            hd_ = g_ * 4 + j_
            dd_ = DILS[g_]
            NBr_ = S // dd_ // 128
            K.dma(qT[:, :], T.dqT[hd_ * 128:(hd_ + 1) * 128, :])
            K.dma(kT[:, :], T.dkT[hd_ * 128:(hd_ + 1) * 128, :])
            for r in range(dd_):
                src = bass.AP(tensor=T.dv_tm.tensor, offset=T.dv_tm.offset + r * DW + hd_ * 128,
                              ap=[[dd_ * DW, 128], [128 * dd_ * DW, NBr_], [1, 128]])
                K.dma(vrs[idx % 2][:, r * NBr_:(r + 1) * NBr_, :], src)

        load_head(0)
        for hidx, (j, g) in enumerate(heads):
            if True:
                hd = g * 4 + j
                dd = DILS[g]
                L = S // dd
                NBr = L // 128
                vr = vrs[hidx % 2]
                nb = min(4, NBr)
                qm, km = qTr, kTr
                for r in range(dd):
                    K.cp(qTr[:, r * L:(r + 1) * L], qT[:, 0:L].mod(1, stride=dd, off=r), eng=K.dve)
                    K.cp(kTr[:, r * L:(r + 1) * L], kT[:, 0:L].mod(1, stride=dd, off=r), eng=K.pool)
                if hidx + 1 < len(heads):
                    load_head(hidx + 1)

                def mcols(t, r, i0, cnt):
                    return t[:, r * L + 128 * i0:r * L + 128 * i0 + cnt]

                def cols(t, r, i0, cnt):
                    return t[:, 0:cnt].mod(1, stride=dd, off=r + dd * 128 * i0)

                items = []
                for r in range(dd):
                    for ib in range(0, NBr, nb):
                        bank = nb_
                        nb_ += 1
                        for ii in range(nb):
                            items.append((r, ib, ii, bank))

                def stage_a(k):
                    r, ib, ii, bank = items[k]
                    i = ib + ii
                    ps_ = pS[k % 2]
                    s_ = sb_[k % 3]
                    p_ = pt[k % 3]
                    os_ = [o for o in range(3) if 0 <= i - 1 + o < NBr]
                    lo, hi = os_[0] * 128, (os_[-1] + 1) * 128
                    for o in os_:
                        kb = i - 1 + o
                        K.mm(ps_[:, o * 128:(o + 1) * 128], mcols(km, r, kb, 128), mcols(qm, r, i, 128),
                             sig=(o == os_[-1]))
                    K.stt(s_[:, lo:hi], ps_[:, lo:hi], ATT_SCALE, Ttab[:, hd, lo:hi], ALU.mult, ALU.add)
                    K.actf(p_[:, lo:hi], s_[:, lo:hi], AF.Exp)

                def stage_b(k):
                    r, ib, ii, bank = items[k]
                    i = ib + ii
                    po, pd = pO[bank % 2], pD[bank % 2]
                    p_ = pt[k % 3]
                    os_ = [o for o in range(3) if 0 <= i - 1 + o < NBr]
                    for o in os_:
                        kb = i - 1 + o
                        K.mm(po[:, ii * 128:(ii + 1) * 128], vr[:, r * NBr + kb, :], p_[:, o * 128:(o + 1) * 128],
                             start=(o == os_[0]), stop=(o == os_[-1]), sig=False)
                    for o in os_:
                        K.mm(pd[:, ii * 128:(ii + 1) * 128], ones[:, :], p_[:, o * 128:(o + 1) * 128],
                             start=(o == os_[0]), stop=(o == os_[-1]), sig=(o == os_[-1]))
                    if ii == nb - 1:
                        w = nb * 128
                        nv = cols(num, r, ib, w)
                        dv_ = cols(den, r, ib, w)
                        if g == 0:
                            K.cp(nv, po[:, :w], eng=K.act)
                            K.cp(dv_, pd[:, :w], eng=K.dve)
                        else:
                            K.tt(nv, nv, po[:, :w], ALU.add)
                            K.tt(dv_, dv_, pd[:, :w], ALU.add, eng=K.dve)

                stage_a(0)
                for k in range(len(items)):
                    if k + 1 < len(items):
                        stage_a(k + 1)
                    stage_b(k)
            for c0 in (range(0, S, 2048) if g == 2 else ()):
                K.recip(den[:, c0:c0 + 2048], den[:, c0:c0 + 2048])
                o_ = ost[(c0 // 2048) % 2]
                K.tt(o_[:, :], num[:, c0:c0 + 2048], den[:, c0:c0 + 2048], ALU.mult)
                K.dma(T.ydilT[j * 128:(j + 1) * 128, c0:c0 + 2048], o_[:, :])
    K.barrier()


def warm(K, ps_tv, lhsT, rhs, n=16):
    for i in range(n):
        K.mm(ps_tv, lhsT, rhs, start=True, stop=True, sig=(i == n - 1))


class TVslice:
    def __init__(self, tile, c0, c1):
        self.tile, self.c0, self.c1 = tile, c0, c1

    def __getitem__(self, idx):
        return self.tile[:, self.c0:self.c1]


class WStream:
    def __init__(self, K, slots, plan):
        self.K, self.slots, self.plan = K, slots, plan
        self.i = 0
        self.n = len(slots)
        for idx in range(min(self.n - 1, len(plan))):
            self._load(idx)

    def _load(self, idx):
        ap, kc0, nkc, c0, w = self.plan[idx]
        slot = self.slots[idx % self.n]
        self.K.dma(slot[:, 0:nkc, 0:w], ap.rearrange("(kc p) n -> p kc n", p=128)[:, kc0:kc0 + nkc, c0:c0 + w])

    def next(self, ap=None):
        idx = self.i
        self.i += 1
        if ap is not None:
            assert self.plan[idx][0] is ap, "weight stream plan mismatch"
        if idx + self.n - 1 < len(self.plan):
            self._load(idx + self.n - 1)
        return self.slots[idx % self.n]


def phase_mix(K, T, l, src_x, dst_x):
    import contextlib
    S = T.S
    NG = S // 512
    W = {k: T.wb[k][l] for k in T.wb}
    with contextlib.ExitStack() as st:
        idf = K.sb(st, "idf", [128, 128], F32)
        K.dma(idf[:, :], T.cmat[:, C_ID:C_ID + 128])
        identb = K.sb(st, "identb", [128, 128], BF16)
        K.cp(identb[:, :], idf[:, :])
        gains = {}
        for nm in ("g_post_mix", "g_pre_ffn", "g_post_ffn", "g_ple"):
            gains[nm] = K.sb(st, nm, [128, D], F32)
            load_bcast(K, gains[nm][:, :], T.v[nm][l])
        wsl = [K.sb(st, f"wsl{i}", [128, 16, 512], BF16) for i in range(3)]
        U = K.sb(st, "U", [128, 36, 512], BF16)
        aT = K.sb(st, "aT", [128, FT, 512], BF16)
        gt = K.sb(st, "gt", [128, 12, 512], BF16)
        xt = [K.sb(st, f"xt{i}", [128, D], F32) for i in range(4)]
        mo = [K.sb(st, f"mo{i}", [128, D], F32) for i in range(4)]
        hT = K.sb(st, "hT", [128, 8, 512], BF16)
        mixT = hT
        pl = K.sb(st, "pl", [128, 4, PLE], F32)
        plb = K.sb(st, "plb", [128, 4, PLE], BF16)
        pTt = K.sb(st, "pTt", [128, 2, 512], BF16)
        sg = K.sb(st, "sg", [128, 3, 512], F32)
        sgl = [K.sb(st, f"sgl{i}", [128, 512], F32) for i in range(4)]
        xn = [K.sb(st, f"xn{i}", [128, D], BF16) for i in range(2)]
        ssx = [K.sb(st, f"ssx{i}", [128, 1], F32) for i in range(2)]
        pb = [K.ps(st, f"pb{i}", [128, 512], F32) for i in range(6)]
        pT = [K.ps(st, f"pT{i}", [128, 8, 128], BF16) for i in range(2)]
        junk = sg[:, 0:2, :]
        macc = [TVslice(mo[i], 0, 512) for i in range(4)]
        cnt = {"ev": 0, "ss": 0, "pb": 0}

        plan1 = []
        for nb in range(2):
            plan1 += [(W["w_br_ssd"], 0, 16, nb * 512, 512), (W["w_br_gqa"], 0, 16, nb * 512, 512),
                      (W["w_br_dil"], 0, 4, nb * 512, 512)]
        plan1 += [(W["w_out"], 0, 8, 0, 512), (W["w_out"], 0, 8, 512, 512)]
        for fb in range(6):
            w = 512 if fb < 5 else 256
            plan1 += [(W["w_gate"], 0, 8, fb * 512, w), (W["w_up"], 0, 8, fb * 512, w)]
        for nb in range(2):
            plan1 += [(W["w_down"], 0, 16, nb * 512, 512), (W["w_down"], 16, 6, nb * 512, 512)]
        for nb in range(2):
            plan1 += [(W["w_ple_gate"], 0, 8, nb * 512, 512), (W["w_ple"], 0, 2, nb * 512, 512)]
        ws = WStream(K, wsl, plan1 * NG)

        def evac(out, in_):
            cnt["ev"] += 1
            K.cp(out, in_, eng=(K.act if cnt["ev"] % 2 else K.dve))

        def rms(src):
            s_ = ssx[cnt["ss"] % 2]
            cnt["ss"] += 1
            K.actf(junk, src.rr("p (a b) -> p a b", a=2) if False else _v3(src, 2, 512), AF.Square, accum=s_[:, :])
            K.ts(s_[:, :], s_[:, :], 1.0 / D, EPS, ALU.mult, ALU.add)
            K.actf(s_[:, :], s_[:, :], AF.Sqrt)
            K.recip(s_[:, :], s_[:, :])
            return s_

        def nextpb():
            cnt["pb"] += 1
            return pb[cnt["pb"] % 6]

        def norm_T(i, gain):
            s_ = rms(xt[i][:, :])
            K.stt(xn[i % 2][:, :], xt[i][:, :], s_[:, :], gain[:, :], ALU.mult, ALU.mult)
            pt = pT[i % 2]
            for kc in range(KD):
                K.tr(pt[:, kc, :], xn[i % 2][:, kc * 128:(kc + 1) * 128], identb[:, :], sig=(kc == KD - 1))
            evac(hT[:, :, i * 128:(i + 1) * 128], pt[:, :, :])

        def post_norm_add(i, gain):
            s_ = rms(mo[i][:, :])
            K.stt(mo[i][:, :], mo[i][:, :], s_[:, :], gain[:, :], ALU.mult, ALU.mult)
            K.tt(xt[i][:, :], xt[i][:, :], mo[i][:, :], ALU.add)

        def load_U(tg_):
            t_ = tg_ * 512
            K.dma(U[:, 0:16, :], T.yssdT[:, t_:t_ + 512].rearrange("(j p) s -> p j s", p=128))
            K.dma(U[:, 16:32, :], T.ygqaT[:, t_:t_ + 512].rearrange("(j p) s -> p j s", p=128))
            K.dma(U[:, 32:36, :], T.ydilT[:, t_:t_ + 512].rearrange("(j p) s -> p j s", p=128))

        for tg in range(NG):
            t0 = tg * 512
            if tg == 0:
                load_U(0)
            for i in range(4):
                K.dma(xt[i][:, :], src_x[t0 + i * 128:t0 + (i + 1) * 128, :])
            K.dma(pl[:, :, :], T.p[l][t0:t0 + 512, :].rearrange("(i p) c -> p i c", p=128))
            warm(K, pb[0][:, :], identb[:, :], U[:, 0, :], n=12)
            for nb in range(2):
                for b in range(3):
                    r0 = b * D + nb * 512
                    K.dma(gt[:, b * 4:(b + 1) * 4, :], T.gatesT[r0:r0 + 512, t0:t0 + 512].rearrange("(j p) s -> p j s", p=128))
                for b, (wname, nkc, u0) in enumerate((("w_br_ssd", 16, 0), ("w_br_gqa", 16, 16), ("w_br_dil", 4, 32))):
                    blk = ws.next(W[wname])
                    for jj in range(4):
                        nt = nb * 4 + jj
                        pA = nextpb()
                        for kc in range(nkc):
                            K.mm(pA[:, :], blk[:, kc, jj * 128:(jj + 1) * 128], U[:, u0 + kc, :], start=(kc == 0),
                                 stop=(kc == nkc - 1), sig=(kc == nkc - 1))
                        sl = sgl[(b * 4 + jj) % 4]
                        K.actf(sl[:, :], gt[:, b * 4 + jj, :], AF.Sigmoid)
                        if b == 0:
                            K.tt(macc[jj][:, :], pA[:, :], sl[:, :], ALU.mult)
                        else:
                            K.tt(sl[:, :], pA[:, :], sl[:, :], ALU.mult)
                            if b == 1:
                                K.tt(macc[jj][:, :], macc[jj][:, :], sl[:, :], ALU.add, eng=K.pool)
                            else:
                                K.tt(mixT[:, nt, :], macc[jj][:, :], sl[:, :], ALU.add, eng=K.pool)
            if tg + 1 < NG:
                load_U(tg + 1)
            for nb in range(2):
                bo = ws.next(W["w_out"])
                for i in range(4):
                    po = nextpb()
                    for kc in range(KD):
                        K.mm(po[:, :], mixT[:, kc, i * 128:(i + 1) * 128], bo[:, kc, :], start=(kc == 0), stop=(kc == KD - 1), sig=(kc == KD - 1))
                    evac(mo[i][:, nb * 512:(nb + 1) * 512], po[:, :])
            if T.debug and tg == 0:
                K.dma(T.dbg_mix.rearrange("(j p) s -> p j s", p=128), mixT[:, :, :])
                for i in range(4):
                    K.dma(T.dbg_mo[i * 128:(i + 1) * 128, :], mo[i][:, :])
            for i in range(4):
                post_norm_add(i, gains["g_post_mix"])
            if T.debug and tg == 0:
                for i in range(4):
                    K.dma(T.dbg_xa[i * 128:(i + 1) * 128, :], xt[i][:, :])
            for i in range(4):
                norm_T(i, gains["g_pre_ffn"])
            warm(K, pb[0][:, :], identb[:, :], hT[:, 0, :], n=12)
            for fb in range(6):
                w = 512 if fb < 5 else 256
                bgt = ws.next(W["w_gate"])
                for jj in range(w // 128):
                    pG = nextpb()
                    for kc in range(KD):
                        K.mm(pG[:, :], bgt[:, kc, jj * 128:(jj + 1) * 128], hT[:, kc, :], start=(kc == 0), stop=(kc == KD - 1), sig=(kc == KD - 1))
                    K.actf(sgl[jj][:, :], pG[:, :], AF.Silu)
                bu = ws.next(W["w_up"])
                for jj in range(w // 128):
                    ft = fb * 4 + jj
                    pU = nextpb()
                    for kc in range(KD):
                        K.mm(pU[:, :], bu[:, kc, jj * 128:(jj + 1) * 128], hT[:, kc, :], start=(kc == 0), stop=(kc == KD - 1), sig=(kc == KD - 1))
                    K.tt(aT[:, ft, :], sgl[jj][:, :], pU[:, :], ALU.mult)
            for nb in range(2):
                b1 = ws.next(W["w_down"])
                for i in range(4):
                    for fc in range(16):
                        K.mm(pb[i][:, :], aT[:, fc, i * 128:(i + 1) * 128], b1[:, fc, :], start=(fc == 0), stop=False, sig=(fc == 15))
                b2 = ws.next(W["w_down"])
                for i in range(4):
                    for fc in range(6):
                        K.mm(pb[i][:, :], aT[:, 16 + fc, i * 128:(i + 1) * 128], b2[:, fc, :], start=False, stop=(fc == 5), sig=(fc == 5))
                    evac(mo[i][:, nb * 512:(nb + 1) * 512], pb[i][:, :])
            for i in range(4):
                post_norm_add(i, gains["g_post_ffn"])
            if T.debug and tg == 0:
                for i in range(4):
                    K.dma(T.dbg_xb[i * 128:(i + 1) * 128, :], xt[i][:, :])
            for i in range(4):
                norm_T(i, gains["g_ple"])
            K.cp(plb[:, :, :], pl[:, :, :], eng=K.pool)
            for i in range(4):
                pt = pT[i % 2]
                for kc in range(2):
                    K.tr(pt[:, kc, :], plb[:, i, kc * 128:(kc + 1) * 128], identb[:, :], sig=(kc == 1))
                evac(pTt[:, :, i * 128:(i + 1) * 128], pt[:, 0:2, :])
            warm(K, pb[5][:, :], identb[:, :], hT[:, 0, :], n=12)
            for nb in range(2):
                hs = slice(nb * 512, (nb + 1) * 512)
                bpg = ws.next(W["w_ple_gate"])
                for i in range(4):
                    for kc in range(KD):
                        K.mm(pb[i][:, :], hT[:, kc, i * 128:(i + 1) * 128], bpg[:, kc, :], start=(kc == 0), stop=(kc == KD - 1), sig=(kc == KD - 1))
                    K.actf(mo[i][:, hs], pb[i][:, :], AF.Sigmoid)
                bp = ws.next(W["w_ple"])
                for i in range(4):
                    po = pb[4 + i % 2]
                    for kc in range(2):
                        K.mm(po[:, :], pTt[:, kc, i * 128:(i + 1) * 128], bp[:, kc, :], start=(kc == 0), stop=(kc == 1), sig=(kc == 1))
                    K.tt(mo[i][:, hs], mo[i][:, hs], po[:, :], ALU.mult)
                    K.tt(xt[i][:, hs], xt[i][:, hs], mo[i][:, hs], ALU.add, eng=K.pool)
            for i in range(4):
                K.dma(dst_x[t0 + i * 128:t0 + (i + 1) * 128, :], xt[i][:, :])
    K.barrier()


SEQ = 8192
N_CORES = 8
_CACHE = {}


def kernel(x_prompt, x_sample, p_prompt, p_sample, **W):
    S = SEQ
    if "nc" not in _CACHE:
        _CACHE["nc"] = build(S)[0]
        _CACHE["consts"] = make_consts(S)
    nc = _CACHE["nc"]
    consts = _CACHE["consts"]
    W = {k: np.ascontiguousarray(np.asarray(v), dtype=np.float32) for k, v in W.items()}
    x_prompt = np.asarray(x_prompt)
    x_sample = np.asarray(x_sample)
    p_prompt = np.asarray(p_prompt)
    p_sample = np.asarray(p_sample)
    nb = x_prompt.shape[0]
    ns = x_sample.shape[0]
    seqs = [("p", i) for i in range(nb)] + [("s", i) for i in range(ns)]
    assert len(seqs) <= N_CORES
    in_maps = []
    zx = np.zeros((S, D), np.float32)
    zp = np.zeros((DEPTH, S, PLE), np.float32)
    for c in range(N_CORES):
        if c < len(seqs):
            kind, i = seqs[c]
            if kind == "p":
                x, p = x_prompt[i], p_prompt[:, i]
            else:
                x, p = x_sample[i], p_sample[:, i]
        else:
            x, p = zx, zp
        in_maps.append(core_inputs(S, x, p, W, consts))
    res = run_bass_kernel_spmd(nc, in_maps, core_ids=list(range(N_CORES)))
    ys = [np.asarray(r["y"], dtype=np.float32) for r in res.results]
    y_prompt = np.stack(ys[:nb], axis=0)
    y_sample = np.stack(ys[nb:nb + ns], axis=0)
    return (y_prompt, y_sample)
```

```python
import math
import numpy as np
import concourse.bass as bass
import concourse.mybir as mybir
from concourse.bass_utils import run_bass_kernel_spmd

F32 = mybir.dt.float32
BF16 = mybir.dt.bfloat16
AF = mybir.ActivationFunctionType
ALU = mybir.AluOpType
AX = mybir.AxisListType

D = 1024
KD = 8
DEPTH = 2
PLE = 256
EPS = 1e-6
NH = 32
HP = 64
DI = 2048
GN = 512
XBC = 3072
HD = 128
QW = 2048
KVW = 512
DW = 1536
DOUT = 512
FF = 2816
FT = 22
IN_TOTAL = 15936
O_Z, O_XBC, O_DT, O_GQ, O_GK, O_GV, O_DQ, O_DK, O_DV, O_GT = 0, 2048, 5120, 5184, 7232, 7744, 8256, 9792, 11328, 12864
ATT_SCALE = HD ** -0.5
NEG = -30000.0
DILS = (1, 4, 16)

DEBUG_STOP = 0
SEM_MAX = 30000
DSEM_MAX = 1800


class Ev:
    __slots__ = ("sem", "val", "key", "eng")

    def __init__(self, sem, val, key, eng=None):
        self.sem, self.val, self.key, self.eng = sem, val, key, eng


class Tile:
    def __init__(self, h, name):
        self.h, self.name = h, name
        self.w = None
        self.r = []
        self.pend = {}

    def __getitem__(self, idx):
        return TV(self, self.h[idx])


class TV:
    __slots__ = ("tile", "ap")

    def __init__(self, tile, ap):
        self.tile, self.ap = tile, ap

    def rr(self, s, **kw):
        return TV(self.tile, self.ap.rearrange(s, **kw))

    def mod(self, dim, stride=None, count=None, off=0):
        a = [list(x) for x in self.ap.ap]
        if stride is not None:
            a[dim][0] = stride
        if count is not None:
            a[dim][1] = count
        return TV(self.tile, bass.AP(tensor=self.ap.tensor, offset=self.ap.offset + off, ap=a))

    def bc(self, dim, n):
        return self.mod(dim, 0, n)


def _ap(x):
    return x.ap if isinstance(x, TV) else x


class Eng:
    def __init__(self, K, name, e):
        self.K, self.name, self.e = K, name, e
        self.sem = None
        self.cnt = 0
        self.epoch = 0
        self.waited = {}
        self.pending = []
        self.last = None

    def wait(self, ev):
        if ev is None:
            return
        if self.waited.get(ev.key, 0) >= ev.val:
            return
        self.e.wait_ge(ev.sem, ev.val)
        self.waited[ev.key] = ev.val

    def signal(self, ins):
        if self.sem is None or self.cnt >= SEM_MAX:
            self.sem = self.K.nc.alloc_semaphore(f"s_{self.name}_{self.epoch}")
            self.epoch += 1
            self.cnt = 0
        self.cnt += 1
        ins.then_inc(self.sem, 1)
        ev = Ev(self.sem, self.cnt, (self.name, self.epoch), self)
        self.last = ev
        return ev


class DmaQ:
    def __init__(self, K, eng, nslots=8, name="q"):
        self.K, self.eng, self.name = K, eng, name
        self.nslots = nslots
        self.slots = [[None, 0, 0] for _ in range(nslots)]
        self.n = 0
        self.outstanding = [None] * nslots

    def issue(self, make):
        k = self.n % self.nslots
        self.n += 1
        sl = self.slots[k]
        if self.outstanding[k] is not None:
            self.eng.wait(self.outstanding[k])
        if sl[0] is None or sl[1] >= DSEM_MAX:
            sl[0] = self.K.nc.alloc_semaphore(f"d_{self.name}_{k}_{sl[2]}")
            sl[2] += 1
            sl[1] = 0
        sl[1] += 1
        ins = make()
        ins.then_inc(sl[0], 16)
        ev = Ev(sl[0], 16 * sl[1], ("d", self.name, k, sl[2]))
        self.outstanding[k] = ev
        return ev


class Kern:
    def __init__(self, nc):
        self.nc = nc
        self.pe = Eng(self, "pe", nc.tensor)
        self.act = Eng(self, "act", nc.scalar)
        self.dve = Eng(self, "dve", nc.vector)
        self.pool = Eng(self, "pool", nc.gpsimd)
        self.sp = Eng(self, "sp", nc.sync)
        self.engs = [self.pe, self.act, self.dve, self.pool, self.sp]
        self.ldq = DmaQ(self, self.sp, 12, "ld")
        self.stq = DmaQ(self, self.sp, 12, "st")
        self.uid = 0
        self.ninstr = 0

    def sb(self, stack, name, shape, dt):
        self.uid += 1
        h = stack.enter_context(self.nc.sbuf_tensor(f"{name}_{self.uid}", list(shape), dt))
        return Tile(h, name)

    def ps(self, stack, name, shape, dt=F32):
        nbytes = int(np.prod(shape[1:])) * (4 if dt == F32 else 2)
        assert nbytes % 2048 == 0, f"PSUM tile {name} must cover whole banks (collisions are HW errors)"
        self.uid += 1
        h = stack.enter_context(self.nc.psum_tensor(f"{name}_{self.uid}", list(shape), dt))
        return Tile(h, name)

    def _deps(self, eng, reads, writes):
        evs = []
        for t in reads:
            for en, n in t.pend.items():
                if n and en is not eng:
                    raise RuntimeError(f"pending unsignaled access on {t.name} by {en.name}")
            if t.w is not None and not (t.w.eng is eng and eng is self.pe):
                evs.append(t.w)
        for t in writes:
            for en, n in t.pend.items():
                if n and en is not eng:
                    raise RuntimeError(f"pending unsignaled access on {t.name} by {en.name}")
            if t.w is not None and t.w.eng is not eng:
                evs.append(t.w)
            evs.extend(e for e in t.r if e.eng is not eng)
        for ev in evs:
            eng.wait(ev)

    def _record(self, ev, reads, writes):
        for t in reads:
            t.r.append(ev)
            if len(t.r) > 24:
                t.r = t.r[-24:] if False else self._compact(t.r)
        for t in writes:
            t.w = ev
            t.r = []

    @staticmethod
    def _compact(evs):
        best = {}
        for ev in evs:
            if ev.key not in best or best[ev.key].val < ev.val:
                best[ev.key] = ev
        return list(best.values())

    def op(self, eng, fn, outs, ins, sig=True):
        reads = [x.tile for x in ins if isinstance(x, TV)]
        writes = [x.tile for x in outs if isinstance(x, TV)]
        self._deps(eng, reads, writes)
        ins_ = fn()
        self.ninstr += 1
        if sig:
            ev = eng.signal(ins_)
            for (t, kind) in eng.pending:
                t.pend[eng] -= 1
                if kind == "r":
                    t.r.append(ev)
                else:
                    t.w = ev
                    t.r = []
            eng.pending = []
            self._record(ev, reads, writes)
            return ev
        for t in reads:
            eng.pending.append((t, "r"))
            t.pend[eng] = t.pend.get(eng, 0) + 1
        for t in writes:
            eng.pending.append((t, "w"))
            t.pend[eng] = t.pend.get(eng, 0) + 1
        return None

    def dma(self, out, in_, q=None, **kw):
        q = q or (self.stq if not isinstance(out, TV) else self.ldq)
        eng = q.eng
        reads = [in_.tile] if isinstance(in_, TV) else []
        writes = [out.tile] if isinstance(out, TV) else []
        self._deps(eng, reads, writes)
        ev = q.issue(lambda: eng.e.dma_start(out=_ap(out), in_=_ap(in_), **kw))
        self.ninstr += 1
        self._record(ev, reads, writes)
        return ev

    def barrier(self):
        evs = []
        for q in (self.ldq, self.stq):
            evs.extend([e for e in q.outstanding if e is not None])
        for en in self.engs:
            if en.pending:
                raise RuntimeError("pending at barrier " + en.name)
            if en.last is not None:
                evs.append(en.last)
        evs = self._compact(evs)
        for en in self.engs:
            for ev in evs:
                en.wait(ev)

    def mm(self, out, lhsT, rhs, start=True, stop=True, sig=True):
        return self.op(self.pe, lambda: self.nc.tensor.matmul(_ap(out), _ap(lhsT), _ap(rhs), start=start, stop=stop),
                       [out], [lhsT, rhs], sig=sig)

    def tr(self, out, in_, ident, sig=True):
        return self.op(self.pe, lambda: self.nc.tensor.transpose(_ap(out), _ap(in_), _ap(ident)),
                       [out], [in_, ident], sig=sig)

    def actf(self, out, in_, func, bias=None, scale=None, accum=None):
        kw = {}
        ins = [in_]
        outs = [out]
        if bias is not None:
            kw["bias"] = _ap(bias)
            if isinstance(bias, TV):
                ins.append(bias)
        if scale is not None:
            kw["scale"] = _ap(scale)
            if isinstance(scale, TV):
                ins.append(scale)
        if accum is not None:
            kw["accum_out"] = _ap(accum)
            outs.append(accum)
        return self.op(self.act, lambda: self.nc.scalar.activation(_ap(out), _ap(in_), func, **kw), outs, ins)

    def _veng(self, eng):
        return eng or self.dve

    def tt(self, out, a, b, op, eng=None):
        eng = self._veng(eng)
        return self.op(eng, lambda: eng.e.tensor_tensor(_ap(out), _ap(a), _ap(b), op), [out], [a, b])

    def ts(self, out, a, s1, s2, op0, op1=None, eng=None):
        eng = self._veng(eng)
        ins = [a] + [s for s in (s1, s2) if isinstance(s, TV)]
        if op1 is None:
            return self.op(eng, lambda: eng.e.tensor_scalar(_ap(out), _ap(a), _ap(s1), None, op0), [out], ins)
        return self.op(eng, lambda: eng.e.tensor_scalar(_ap(out), _ap(a), _ap(s1), _ap(s2), op0, op1), [out], ins)

    def stt(self, out, a, s, b, op0, op1):
        ins = [a, b] + ([s] if isinstance(s, TV) else [])
        return self.op(self.dve, lambda: self.nc.vector.scalar_tensor_tensor(_ap(out), _ap(a), _ap(s), _ap(b), op0, op1),
                       [out], ins)

    def cp(self, out, in_, eng=None):
        eng = self._veng(eng)
        if eng is self.act:
            return self.op(eng, lambda: self.nc.scalar.copy(_ap(out), _ap(in_)), [out], [in_])
        return self.op(eng, lambda: eng.e.tensor_copy(_ap(out), _ap(in_)), [out], [in_])

    def recip(self, out, in_):
        return self.op(self.dve, lambda: self.nc.vector.reciprocal(_ap(out), _ap(in_)), [out], [in_])

    def rsum(self, out, in_):
        return self.op(self.dve, lambda: self.nc.vector.reduce_sum(_ap(out), _ap(in_), AX.X), [out], [in_])

    def mset(self, out, val, eng=None):
        eng = self._veng(eng)
        return self.op(eng, lambda: eng.e.memset(_ap(out), val), [out], [])


C_ID, C_U, C_LS, C_L, C_US, C_ONE, C_BT = 0, 128, 256, 384, 512, 640, 768
CW = 768 + 3 * 384


def _t5_bucket_np(rel):
    nb = 16
    max_exact = 8
    ret = np.where(rel > 0, nb, 0)
    n = np.abs(rel)
    nf = np.maximum(n, 1).astype(np.float32)
    large = max_exact + (np.log(nf / np.float32(max_exact)) / np.float32(math.log(2048 / max_exact))
                         * np.float32(nb - max_exact)).astype(np.int32)
    large = np.minimum(large, nb - 1)
    return ret + np.where(n < max_exact, n, large)


def make_consts(S):
    c = np.zeros((128, CW), np.float32)
    p = np.arange(128)[:, None]
    j = np.arange(128)[None, :]
    c[:, C_ID:C_ID + 128] = (p == j)
    c[:, C_U:C_U + 128] = (p <= j)
    c[:, C_LS:C_LS + 128] = (p > j)
    c[:, C_L:C_L + 128] = (p >= j)
    c[:, C_US:C_US + 128] = (p < j)
    c[:, C_ONE:C_ONE + 128] = 1.0
    for g, dd in enumerate(DILS):
        k = np.arange(128)[:, None, None]
        o = np.arange(3)[None, :, None]
        q = np.arange(128)[None, None, :]
        rel = k + 128 * (o - 1) - q
        bk = _t5_bucket_np(rel * dd).astype(np.float32)
        bk = np.where(np.abs(rel) <= 64, bk, -1.0)
        c[:, C_BT + g * 384:C_BT + (g + 1) * 384] = bk.reshape(128, 384)
    t = np.arange(S)
    row = (t // 64).astype(np.float32)
    col = (t % 64).astype(np.float32)
    inv = (np.float32(10000.0) ** (-np.arange(32, dtype=np.float32) / np.float32(32))).astype(np.float32)
    ang = np.concatenate([row[:, None] * inv, col[:, None] * inv], axis=-1).astype(np.float32)
    cs = np.concatenate([np.cos(ang), np.sin(ang)], axis=-1).astype(np.float32)
    return c, cs


WSPECS = [("w_in", D, IN_TOTAL), ("w_br_ssd", DI, D), ("w_br_gqa", QW, D), ("w_br_dil", DOUT, D), ("w_out", D, D),
          ("w_gate", D, FF), ("w_up", D, FF), ("w_down", FF, D), ("w_ple", PLE, D), ("w_ple_gate", D, D)]
VSPECS = [("conv_w", [DEPTH, 5, XBC]), ("conv_b", [DEPTH, XBC]), ("dt_bias", [DEPTH, 2, NH]), ("a_log", [DEPTH, 2, NH]),
          ("d_skip", [DEPTH, NH]), ("g_ssd", [DEPTH, DI]), ("g_q", [DEPTH, HD]), ("g_k", [DEPTH, HD]),
          ("g_pre_mix", [DEPTH, D]), ("g_post_mix", [DEPTH, D]), ("g_pre_ffn", [DEPTH, D]), ("g_post_ffn", [DEPTH, D]),
          ("g_ple", [DEPTH, D]), ("rel_bias", [32, 12])]


def bcast_rows(ap1d, n=128):
    a = [list(x) for x in ap1d.ap]
    return bass.AP(tensor=ap1d.tensor, offset=ap1d.offset, ap=[[0, n]] + a)


class Ctx:
    pass


def declare(nc, S, debug):
    T = Ctx()
    T.S = S
    ein = lambda name, shape, dt=F32: nc.dram_tensor(name, list(shape), dt, kind="ExternalInput").ap()
    T.x = ein("x", [S, D])
    T.p = ein("p", [DEPTH, S, PLE])
    T.w = {}
    for name, k, n in WSPECS:
        T.w[name] = ein(name, [DEPTH, k, n])
    T.v = {}
    for name, shape in VSPECS:
        T.v[name] = ein(name, shape)
    T.cmat = ein("cmat", [128, CW])
    T.cs = ein("cs", [S, 128])
    T.y = nc.dram_tensor("y", [S, D], F32, kind="ExternalOutput").ap()
    kind = "ExternalOutput" if debug else "Internal"
    scr = lambda name, shape, dt=BF16: nc.dram_tensor(name, list(shape), dt, kind=kind).ap()
    T.wb = {}
    for name, k, n in WSPECS:
        T.wb[name] = [nc.dram_tensor(f"wb_{name}_{l}", [k, n], BF16, kind="Internal").ap() for l in range(DEPTH)]
    T.xres = scr("xres", [S, D], F32)
    T.z_tm = scr("z_tm", [S, DI])
    T.xbcT = scr("xbcT", [XBC, S])
    T.dt_tm = scr("dt_tm", [S, 64], F32)
    T.da_tm = scr("da_tm", [S, 64], F32)
    T.qT = scr("qT", [QW, S])
    T.kT = scr("kT", [KVW, S])
    T.v_tm = scr("v_tm", [S, KVW])
    T.dqT = scr("dqT", [DW, S])
    T.dkT = scr("dkT", [DW, S])
    T.dv_tm = scr("dv_tm", [S, DW])
    T.gatesT = scr("gatesT", [3 * D, S])
    T.xs_tm = scr("xs_tm", [S, DI])
    T.b_tm = scr("b_tm", [S, GN])
    T.bcT = scr("bcT", [2 * GN, S])
    T.hb = scr("hb", [S // 128, 128, DI])
    T.yssdT = scr("yssdT", [DI, S])
    T.ygqaT = scr("ygqaT", [QW, S])
    T.ydilT = scr("ydilT", [DOUT, S])
    T.ttab = nc.dram_tensor("ttab", [128, 12 * 384], F32, kind="Internal").ap()
    T.debug = debug
    if debug:
        T.dbg_xa = scr("dbg_xa", [512, D], F32)
        T.dbg_xb = scr("dbg_xb", [512, D], F32)
        T.dbg_mo = scr("dbg_mo", [512, D], F32)
        T.dbg_mix = scr("dbg_mix", [D, 512], BF16)
    return T


WDIMS = {name: (k, n) for name, k, n in WSPECS}


def wconv_gen(K, T, st, items, engs, NB=3):
    fin = [K.sb(st, f"wc_f{i}", [128, 2048], F32) for i in range(NB)]
    fo = [K.sb(st, f"wc_b{i}", [128, 2048], BF16) for i in range(NB)]
    work = []
    for name, l in items:
        k, ncols = WDIMS[name]
        for r in range(k // 128):
            for c0 in range(0, ncols, 2048):
                w = min(2048, ncols - c0)
                work.append((T.w[name][l][r * 128:(r + 1) * 128, c0:c0 + w], T.wb[name][l][r * 128:(r + 1) * 128, c0:c0 + w], w))

    def g():
        n = len(work)
        for k in range(n + 2):
            if k - 2 >= 0:
                _, dst, w = work[k - 2]
                K.dma(dst, fo[(k - 2) % NB][:, :w])
            if 0 <= k - 1 < n:
                _, _, w = work[k - 1]
                K.cp(fo[(k - 1) % NB][:, :w], fin[(k - 1) % NB][:, :w], eng=engs[(k - 1) % len(engs)])
            if k < n:
                src, _, w = work[k]
                K.dma(fin[k % NB][:, :w], src)
            yield
    return g()


def phase_wconv(K, T, items, with_ttab=False):
    import contextlib
    with contextlib.ExitStack() as st:
        gen = wconv_gen(K, T, st, items, [K.pool, K.act] if with_ttab else [K.dve, K.pool, K.act])
        if with_ttab:
            cm = K.sb(st, "cm", [128, CW], F32)
            K.dma(cm[:, :], T.cmat[:, :])
            rb = K.sb(st, "rb", [128, 384], F32)
            load_bcast(K, rb[:, :], T.v["rel_bias"].rearrange("a b -> (a b)"))
            Ttab = K.sb(st, "Ttab", [128, 12, 384], F32)
            tmpb = [K.sb(st, f"tmpb{i}", [128, 384], F32) for i in range(2)]
            for hd in range(12):
                g = hd // 4
                BT = cm[:, C_BT + g * 384:C_BT + (g + 1) * 384]
                K.ts(Ttab[:, hd, :], BT, 0.0, NEG, ALU.is_lt, ALU.mult)
                for b in range(32):
                    tb = tmpb[b % 2]
                    K.ts(tb[:, :], BT, float(b), rb[:, b * 12 + hd:b * 12 + hd + 1], ALU.is_equal, ALU.mult)
                    K.tt(Ttab[:, hd, :], Ttab[:, hd, :], tb[:, :], ALU.add)
                    if b % 4 == 3:
                        next(gen, None)
            K.dma(T.ttab.rearrange("p (h c) -> p h c", h=12), Ttab[:, :, :])
        for _ in gen:
            pass
    K.barrier()


def load_bcast(K, tile_tv, vec_ap):
    K.dma(tile_tv, bcast_rows(vec_ap))


def phase_inproj(K, T, l, src_x, bg_items=None):
    import contextlib
    S = T.S
    NG = S // 512
    with contextlib.ExitStack() as st:
        bg = wconv_gen(K, T, st, bg_items, [K.pool]) if bg_items else None
        cm = K.sb(st, "cm", [128, CW], F32)
        K.dma(cm[:, :], T.cmat[:, :])
        identb = K.sb(st, "identb", [128, 128], BF16)
        K.cp(identb[:, :], cm[:, C_ID:C_ID + 128])
        gpre = K.sb(st, "gpre", [128, D], F32)
        load_bcast(K, gpre[:, :], T.v["g_pre_mix"][l])
        gq = K.sb(st, "gq", [128, HD], F32)
        load_bcast(K, gq[:, :], T.v["g_q"][l])
        gk = K.sb(st, "gk", [128, HD], F32)
        load_bcast(K, gk[:, :], T.v["g_k"][l])
        dtb = K.sb(st, "dtb", [128, 64], F32)
        load_bcast(K, dtb[:, :], T.v["dt_bias"][l].rearrange("a b -> (a b)"))
        atab = K.sb(st, "atab", [128, 64], F32)
        load_bcast(K, atab[:, :], T.v["a_log"][l].rearrange("a b -> (a b)"))
        K.actf(atab[:, :], atab[:, :], AF.Exp)
        K.ts(atab[:, :], atab[:, :], -1.0, None, ALU.mult)

        xt = [K.sb(st, f"xt{i}", [128, D], F32) for i in range(2)]
        xn = [K.sb(st, f"xn{i}", [128, D], BF16) for i in range(2)]
        junk = K.sb(st, "junk", [128, D], F32)
        ss = [K.sb(st, f"ss{i}", [128, 1], F32) for i in range(2)]
        psT = [K.ps(st, f"psT{i}", [128, KD, 128], BF16) for i in range(2)]
        hT = [K.sb(st, f"hT{i}", [128, KD, 512], BF16) for i in range(2)]
        NW = 3
        wblk = [K.sb(st, f"wblk{i}", [128, KD, 512], BF16) for i in range(NW)]
        pso = [K.ps(st, f"pso{i}", [128, 512], F32) for i in range(4)]
        psQ = [K.ps(st, f"psQ{i}", [128, 8, 128], BF16) for i in range(2)]
        NSO = 6
        so = [K.sb(st, f"so{i}", [128, 4, 512], BF16) for i in range(NSO)]
        sodt = K.sb(st, "sodt", [128, 4, 64], F32)
        soda = K.sb(st, "soda", [128, 4, 64], F32)
        tmpdt = K.sb(st, "tmpdt", [128, 64], F32)
        cst = [K.sb(st, f"cst{i}", [128, 4, 128], F32) for i in range(2)]
        sq = [K.sb(st, f"sq{i}", [128, 512], F32) for i in range(2)]
        qss = [K.sb(st, f"qss{i}", [128, 4], F32) for i in range(4)]
        qf = [K.sb(st, f"qf{i}", [128, 512], F32) for i in range(4)]
        rt = [[K.sb(st, f"rt{j}_{i}", [128, 4, 64], F32) for i in range(4)] for j in range(2)]
        qr = [K.sb(st, f"qr{i}", [128, 512], BF16) for i in range(4)]
        deferred = []

        def blk(kind, base, dst, n):
            return [(base + 512 * b_, 512, kind, dst, 512 * b_) for b_ in range(n)]

        zb = blk("tm", O_Z, T.z_tm, 4)
        xb = blk("fm", O_XBC, T.xbcT, 6)
        qb = blk("qk", O_GQ, T.qT, 4)
        kb = [(O_GK, 512, "qk", T.kT, 0)]
        vb = [(O_GV, 512, "tm", T.v_tm, 0)]
        dqb = blk("fm", O_DQ, T.dqT, 3)
        dkb = blk("fm", O_DK, T.dkT, 3)
        dvb = blk("tm", O_DV, T.dv_tm, 3)
        gb = blk("fm", O_GT, T.gatesT, 6)
        dtb_ = [(O_DT, 64, "dt", None, 0)]
        others = zb + xb + dtb_ + vb + dqb + dkb + dvb + gb
        qks = qb + kb
        blocks = []
        per = len(others) // len(qks)
        oi = 0
        for qi, qk_ in enumerate(qks):
            blocks.append(qk_)
            take = per if qi < len(qks) - 1 else len(others) - oi
            blocks += others[oi:oi + take]
            oi += take
        NB = len(blocks)
        wsrc = T.wb["w_in"][l].rearrange("(kc p) n -> p kc n", p=128)

        seq = [(tg, bi) for tg in range(NG) for bi in range(NB)]

        def load_w(idx):
            tg, bi = seq[idx]
            c0, w = blocks[bi][0], blocks[bi][1]
            K.dma(wblk[idx % NW][:, :, :w], wsrc[:, :, c0:c0 + w])

        cnt = {"pso": 0, "so": 0, "ev": 0, "q": 0}

        def evac(out, in_):
            cnt["ev"] += 1
            if cnt["ev"] % 3:
                K.cp(out, in_, eng=K.act)
            else:
                K.cp(out, in_, eng=K.dve)

        load_w(0)
        load_w(1)
        for tg in range(NG):
            t0 = tg * 512
            h = hT[tg % 2]
            c_t = cst[tg % 2]
            K.dma(c_t[:, :, :], T.cs[t0:t0 + 512, :].rearrange("(i p) c -> p i c", p=128))
            for i in range(4):
                xi = xt[i % 2]
                K.dma(xi[:, :], src_x[t0 + i * 128:t0 + (i + 1) * 128, :])
                s_ = ss[i % 2]
                K.actf(junk[:, :], xi[:, :], AF.Square, accum=s_[:, :])
                K.ts(s_[:, :], s_[:, :], 1.0 / D, EPS, ALU.mult, ALU.add)
                K.actf(s_[:, :], s_[:, :], AF.Sqrt)
                K.recip(s_[:, :], s_[:, :])
                K.stt(xn[i % 2][:, :], xi[:, :], s_[:, :], gpre[:, :], ALU.mult, ALU.mult)
                pt = psT[i % 2]
                for kc in range(KD):
                    K.tr(pt[:, kc, :], xn[i % 2][:, kc * 128:(kc + 1) * 128], identb[:, :], sig=(kc == KD - 1))
                evac(h[:, :, i * 128:(i + 1) * 128], pt[:, :, :])
            warm(K, pso[cnt["pso"] % 4][:, :], identb[:, :], h[:, 0, :], n=12)
            for bi in range(NB):
                idx = tg * NB + bi
                if idx + 2 < len(seq):
                    load_w(idx + 2)
                wb_ = wblk[idx % NW]
                c0, w, kind, dst, doff = blocks[bi]
                if bg is not None:
                    next(bg, None)
                if kind == "tm":
                    s_o = so[cnt["so"] % NSO]
                    cnt["so"] += 1
                    for i in range(4):
                        po = pso[cnt["pso"] % 4]
                        cnt["pso"] += 1
                        for kc in range(KD):
                            K.mm(po[:, :], h[:, kc, i * 128:(i + 1) * 128], wb_[:, kc, :], start=(kc == 0),
                                 stop=(kc == KD - 1), sig=(kc == KD - 1))
                        evac(s_o[:, i, :], po[:, :])
                    K.dma(dst[t0:t0 + 512, doff:doff + 512].rearrange("(i p) n -> p i n", p=128), s_o[:, :, :])
                elif kind == "fm":
                    s_o = so[cnt["so"] % NSO]
                    cnt["so"] += 1
                    for j in range(4):
                        po = pso[cnt["pso"] % 4]
                        cnt["pso"] += 1
                        for kc in range(KD):
                            K.mm(po[:, :], wb_[:, kc, j * 128:(j + 1) * 128], h[:, kc, :], start=(kc == 0),
                                 stop=(kc == KD - 1), sig=(kc == KD - 1))
                        evac(s_o[:, j, :], po[:, :])
                    K.dma(dst[doff:doff + 512, t0:t0 + 512].rearrange("(j p) s -> p j s", p=128), s_o[:, :, :])
                elif kind == "dt":
                    for i in range(4):
                        po = pso[cnt["pso"] % 4]
                        cnt["pso"] += 1
                        for kc in range(KD):
                            K.mm(po[:, :64], h[:, kc, i * 128:(i + 1) * 128], wb_[:, kc, :64], start=(kc == 0),
                                 stop=(kc == KD - 1), sig=(kc == KD - 1))
                        K.tt(tmpdt[:, :], po[:, :64], dtb[:, :], ALU.add)
                        K.actf(tmpdt[:, :], tmpdt[:, :], AF.Exp)
                        K.actf(sodt[:, i, :], tmpdt[:, :], AF.Ln, bias=1.0)
                        K.tt(soda[:, i, :], sodt[:, i, :], atab[:, :], ALU.mult)
                    K.dma(T.dt_tm[t0:t0 + 512, :].rearrange("(i p) n -> p i n", p=128), sodt[:, :, :])
                    K.dma(T.da_tm[t0:t0 + 512, :].rearrange("(i p) n -> p i n", p=128), soda[:, :, :])
                elif kind == "qk":
                    gtab = gk if dst is T.kT else gq
                    s_o = so[cnt["so"] % NSO]
                    cnt["so"] += 1
                    qrs = []
                    for i in range(4):
                        po = pso[cnt["pso"] % 4]
                        cnt["pso"] += 1
                        for kc in range(KD):
                            K.mm(po[:, :], h[:, kc, i * 128:(i + 1) * 128], wb_[:, kc, :], start=(kc == 0),
                                 stop=(kc == KD - 1), sig=(kc == KD - 1))
                        qi = cnt["q"] % 4
                        cnt["q"] += 1
                        K.cp(qf[qi][:, :], po[:, :], eng=K.act)
                        K.actf(sq[qi % 2][:, :], qf[qi][:, :], AF.Square)
                        K.rsum(qss[qi][:, :], sq[qi % 2][:, :].rr("p (h d) -> p h d", h=4))
                        K.ts(qss[qi][:, :], qss[qi][:, :], 1.0 / HD, EPS, ALU.mult, ALU.add)
                        K.actf(qss[qi][:, :], qss[qi][:, :], AF.Sqrt)
                        K.recip(qss[qi][:, :], qss[qi][:, :])
                        q3 = qf[qi][:, :].rr("p (h d) -> p h d", h=4)
                        K.tt(q3, q3, qss[qi][:, :].rr("p (h o) -> p h o", o=1).bc(2, HD), ALU.mult)
                        K.tt(q3, q3, gtab[:, :].rr("p (o d) -> p o d", o=1).bc(1, 4), ALU.mult)
                        q4 = qf[qi][:, :].rr("p (h d two) -> p h d two", h=4, two=2)
                        x0 = TV(q4.tile, q4.ap[:, :, :, 0])
                        x1 = TV(q4.tile, q4.ap[:, :, :, 1])
                        cc = c_t[:, i, 0:64].rr("p (o d) -> p o d", o=1).bc(1, 4)
                        sn = c_t[:, i, 64:128].rr("p (o d) -> p o d", o=1).bc(1, 4)
                        o4 = qr[qi][:, :].rr("p (h d two) -> p h d two", h=4, two=2)
                        o0 = TV(o4.tile, o4.ap[:, :, :, 0])
                        o1 = TV(o4.tile, o4.ap[:, :, :, 1])
                        r_ = rt[qi % 2]
                        K.tt(r_[0][:, :, :], x0, cc, ALU.mult)
                        K.tt(r_[1][:, :, :], x1, sn, ALU.mult)
                        K.tt(o0, r_[0][:, :, :], r_[1][:, :, :], ALU.subtract)
                        K.tt(r_[2][:, :, :], x0, sn, ALU.mult, eng=K.pool)
                        K.tt(r_[3][:, :, :], x1, cc, ALU.mult, eng=K.pool)
                        K.tt(o1, r_[2][:, :, :], r_[3][:, :, :], ALU.add, eng=K.pool)
                        qrs.append(qr[qi])

                    def fin(qrs=qrs, s_o=s_o, dst=dst, doff=doff, t0=t0):
                        for i, qr_ in enumerate(qrs):
                            pq = psQ[i % 2]
                            for hh in range(4):
                                K.tr(pq[:, hh, :], qr_[:, hh * 128:(hh + 1) * 128], identb[:, :], sig=(hh == 3))
                            evac(s_o[:, :, i * 128:(i + 1) * 128], pq[:, 0:4, :])
                        K.dma(dst[doff:doff + 512, t0:t0 + 512].rearrange("(h d) s -> d h s", d=128), s_o[:, :, :])

                    deferred.append([3, fin])
                if kind != "qk":
                    for dfr in deferred:
                        dfr[0] -= 1
                    while deferred and deferred[0][0] <= 0:
                        deferred.pop(0)[1]()
            while deferred:
                deferred.pop(0)[1]()
        if bg is not None:
            for _ in bg:
                pass
    K.barrier()


def build(S, debug=False, phases=None):
    nc = bass.Bass("TRN2", target_bir_lowering=False)
    T = declare(nc, S, debug)
    K = Kern(nc)
    ph = phases or ["all"]
    allp = "all" in ph
    allw = [(name, l) for l in range(DEPTH) for name, _, _ in WSPECS]
    if allp:
        phase_wconv(K, T, [("w_in", 0)], with_ttab=True)
        bg_items = [it for it in allw if it != ("w_in", 0)]
        bg_items1 = None
    else:
        bg_items = None
        bg_items1 = None
        if "wconv" in ph:
            phase_wconv(K, T, allw, with_ttab=True)
    for l in range(DEPTH):
        src_x = T.x if l == 0 else T.xres
        dst_x = T.xres if l < DEPTH - 1 else T.y
        if allp or "inproj" in ph:
            phase_inproj(K, T, l, src_x, None)
        if allp or "ssd" in ph:
            phase_ssd(K, T, l, bg_items1 if l == 0 else None)
        if allp or "gqa" in ph:
            phase_gqa(K, T, l, bg_items if l == 0 else None)
        if allp or "dil" in ph:
            phase_dil(K, T, l)
        if allp or "mix" in ph:
            phase_mix(K, T, l, src_x, dst_x)
        if not allp and "onelayer" in ph:
            break
    K.barrier()
    return nc, K


def core_inputs(S, x, p, W, consts):
    cmat, cs = consts
    m = {"x": np.ascontiguousarray(x, dtype=np.float32), "p": np.ascontiguousarray(p, dtype=np.float32),
         "cmat": cmat, "cs": cs}
    for name, _, _ in WSPECS:
        m[name] = W[name]
    for name, _ in VSPECS:
        m[name] = W[name]
    return m


def _v3(tv, a, b):
    return tv.rr("p (a b) -> p a b", a=a, b=b)


def _bl(tv, n):
    return tv.rr("p (a o) -> p a o", o=1).bc(2, n)


def _bm(tv, n):
    return tv.rr("p (o b) -> p o b", o=1).bc(1, n)


def phase_ssd(K, T, l, bg_items=None):
    import contextlib
    S = T.S
    NG = S // 512
    NT = S // 128
    with contextlib.ExitStack() as st:
        cm = K.sb(st, "cm", [128, CW], F32)
        K.dma(cm[:, :], T.cmat[:, :])
        identf = cm[:, C_ID:C_ID + 128]
        identb = K.sb(st, "identb", [128, 128], BF16)
        K.cp(identb[:, :], identf)
        cwb = K.sb(st, "cwb", [6, XBC], F32)
        K.dma(cwb[0:5, :], T.v["conv_w"][l])
        K.dma(cwb[5:6, :], T.v["conv_b"][l].rearrange("(o n) -> o n", o=1))
        pcw = K.ps(st, "pcw", [128, 64, 8], F32)
        for j in range(24):
            K.tr(pcw[:, j, 0:6], cwb[0:6, j * 128:(j + 1) * 128], cm[0:6, C_ID:C_ID + 6], sig=(j == 23))
        cwT = K.sb(st, "cwT", [128, 24, 8], F32)
        K.cp(cwT[:, :, 0:6], pcw[:, 0:24, 0:6])
        dg = K.sb(st, "dg", [128, 24 * 5, 128], BF16)
        for j in range(24):
            for k in range(5):
                K.ts(dg[:, j * 5 + k, :], identf, cwT[:, j, k:k + 1], None, ALU.mult,
                     eng=(K.dve if (j * 5 + k) % 2 else K.pool))
        xin = [K.sb(st, f"xin{i}", [128, 24, 516], BF16) for i in range(2)]
        xo = [K.sb(st, f"xo{i}", [128, 24, 512], BF16) for i in range(2)]
        pso = [K.ps(st, f"pc{i}", [128, 512], F32) for i in range(3)]
        pT = [K.ps(st, f"pT{i}", [128, 8, 128], BF16) for i in range(2)]
        xst = K.sb(st, "xst", [128, 4, DI], BF16)
        bst = K.sb(st, "bst", [128, 4, GN], BF16)
        srcv = T.xbcT.rearrange("(j p) s -> p j s", p=128)
        n_pt = 0
        for tg in range(NG):
            t0 = tg * 512
            xi = xin[tg % 2]
            lo = 2 if tg == 0 else 0
            hi = 514 if tg == NG - 1 else 516
            if tg == 0:
                K.mset(xi[:, :, 0:2], 0.0)
            if tg == NG - 1:
                K.mset(xi[:, :, 514:516], 0.0)
            K.dma(xi[:, :, lo:hi], srcv[:, :, t0 - 2 + lo:t0 - 2 + hi])
            xo_ = xo[tg % 2]
            for j in range(24):
                pc = pso[j % 3]
                for k in range(5):
                    K.mm(pc[:, :], dg[:, j * 5 + k, :], xi[:, j, k:k + 512], start=(k == 0), stop=(k == 4), sig=(k == 4))
                K.actf(xo_[:, j, :], pc[:, :], AF.Silu, bias=cwT[:, j, 5:6])
            K.dma(T.bcT[:, t0:t0 + 512].rearrange("(j p) s -> p j s", p=128), xo_[:, 16:24, :])
            for i in range(4):
                for rnd in range(3):
                    pt = pT[n_pt % 2]
                    n_pt += 1
                    nj = 8 if rnd < 2 else 4
                    for jj in range(nj):
                        j = rnd * 8 + jj
                        K.tr(pt[:, jj, :], xo_[:, j, i * 128:(i + 1) * 128], identb[:, :], sig=(jj == nj - 1))
                    if rnd < 2:
                        K.cp(_v3(xst[:, i, rnd * 1024:(rnd + 1) * 1024], 8, 128), pt[:, :, :],
                             eng=(K.dve if rnd else K.act))
                    else:
                        K.cp(_v3(bst[:, i, :], 4, 128), pt[:, 0:4, :], eng=K.dve)
            K.dma(T.xs_tm[t0:t0 + 512, :].rearrange("(i p) n -> p i n", p=128), xst[:, :, :])
            K.dma(T.b_tm[t0:t0 + 512, :].rearrange("(i p) n -> p i n", p=128), bst[:, :, :])
    K.barrier()

    with contextlib.ExitStack() as st:
        cm = K.sb(st, "cm", [128, CW], F32)
        K.dma(cm[:, :], T.cmat[:, :])
        xs_c = [K.sb(st, f"xs{i}", [128, DI], BF16) for i in range(2)]
        b_c = [K.sb(st, f"bc{i}", [128, GN], BF16) for i in range(2)]
        dtda = [K.sb(st, f"dtda{i}", [128, 128], F32) for i in range(2)]
        pss = K.ps(st, "pss", [128, 512], F32)
        ew = [K.sb(st, f"ew{i}", [128, 64], F32) for i in range(2)]
        wsc = [K.sb(st, f"wsc{i}", [128, 32], F32) for i in range(2)]
        xcs = [K.sb(st, f"xcs{i}", [128, DI], BF16) for i in range(2)]
        pst = K.ps(st, "pst", [128, DI], F32)
        H = K.sb(st, "H", [128, DI], F32)
        Hbf = [K.sb(st, f"Hbf{i}", [128, DI], BF16) for i in range(2)]
        K.mset(H[:, :], 0.0)
        bg = wconv_gen(K, T, st, bg_items, [K.pool, K.act]) if bg_items else None
        nbg = (sum((WDIMS[nm][0] // 128) * ((WDIMS[nm][1] + 2047) // 2048) for nm, _ in bg_items) + NT - 1) // NT if bg_items else 0
        for n, c in enumerate(range(NT - 1, -1, -1)):
            r0 = c * 128
            a = n % 2
            for _ in range(nbg):
                next(bg, None)
            K.dma(xs_c[a][:, :], T.xs_tm[r0:r0 + 128, :])
            K.dma(b_c[a][:, :], T.b_tm[r0:r0 + 128, :])
            K.dma(dtda[a][:, 0:64], T.dt_tm[r0:r0 + 128, :])
            K.dma(dtda[a][:, 64:128], T.da_tm[r0:r0 + 128, :])
            da_b = dtda[a][:, 96:128]
            dt_b = dtda[a][:, 32:64]
            K.mm(pss[:, 0:32], cm[:, C_US:C_US + 128], da_b, sig=False)
            K.mm(pss[:, 32:64], cm[:, C_ONE:C_ONE + 128], da_b)
            K.actf(ew[a][:, :], pss[:, 0:64], AF.Exp)
            K.tt(wsc[a][:, :], ew[a][:, 0:32], dt_b, ALU.mult)
            K.tt(_v3(xcs[a][:, :], NH, HP), _v3(xs_c[a][:, :], NH, HP), _bl(wsc[a][:, :], HP), ALU.mult)
            K.cp(Hbf[a][:, :], H[:, :], eng=K.act)
            K.dma(T.hb[c], Hbf[a][:, :])
            for g in range(4):
                K.mm(pst[:, g * 512:(g + 1) * 512], b_c[a][:, g * 128:(g + 1) * 128], xcs[a][:, g * 512:(g + 1) * 512],
                     sig=(g == 3))
            K.tt(_v3(H[:, :], NH, HP), _v3(H[:, :], NH, HP), _bl(ew[a][:, 32:64], HP), ALU.mult)
            K.tt(H[:, :], H[:, :], pst[:, :], ALU.add)
        if bg is not None:
            for _ in bg:
                pass
    K.barrier()

    with contextlib.ExitStack() as st:
        cm = K.sb(st, "cm", [128, CW], F32)
        K.dma(cm[:, :], T.cmat[:, :])
        identb = K.sb(st, "identb", [128, 128], BF16)
        K.cp(identb[:, :], cm[:, C_ID:C_ID + 128])
        gssd = K.sb(st, "gssd", [128, DI], F32)
        load_bcast(K, gssd[:, :], T.v["g_ssd"][l])
        dsk = K.sb(st, "dsk", [128, NH], F32)
        load_bcast(K, dsk[:, :], T.v["d_skip"][l])
        two = lambda name, shape, dt: [K.sb(st, f"{name}{i}", shape, dt) for i in range(2)]
        xs_c = two("xs", [128, DI], BF16)
        b_c = two("bc", [128, GN], BF16)
        bcT_c = two("bcT", [128, 8, 128], BF16)
        dtda = two("dtda", [128, 128], F32)
        z_c = two("z", [128, DI], BF16)
        hb_c = two("hb", [128, DI], BF16)
        ew = two("ew", [128, 256], F32)
        wf = two("wf", [128, 32], F32)
        xc_f = two("xc_f", [128, DI], BF16)
        xc_b = two("xc_b", [128, DI], BF16)
        xcs_f = two("xcs_f", [128, DI], BF16)
        cbf = two("cbf", [128, 512], F32)
        cbb = two("cbb", [128, 512], F32)
        Rg = [two("Rf", [128, 8 * 128], F32), two("Rb", [128, 8 * 128], F32)]
        Hfb = two("Hfb", [128, DI], BF16)
        pss = K.ps(st, "pss", [128, 512], F32)
        psh = K.ps(st, "psh", [128, 512], F32)
        pseg = [K.ps(st, f"pseg{i}", [128, 512], F32) for i in range(2)]
        py = K.ps(st, "py", [128, DI], F32)
        ebuf = [K.sb(st, f"ebuf{i}", [128, 512], F32) for i in range(3)]
        mT = [K.sb(st, f"mT{i}", [128, 512], BF16) for i in range(3)]
        yacc = K.sb(st, "yacc", [128, DI], F32)
        ytmp = K.sb(st, "ytmp", [128, DI], F32)
        sz = K.sb(st, "sz", [128, DI], F32)
        ssq = K.sb(st, "ssq", [128, 1], F32)
        yn = K.sb(st, "yn", [128, DI], BF16)
        Hf = K.sb(st, "Hf", [128, DI], F32)
        yTs = K.sb(st, "yTs", [128, 16, 512], BF16)
        K.mset(Hf[:, :], 0.0)
        K.mset(Hfb[0][:, :], 0.0)
        bcv = T.bcT.rearrange("(j p) s -> p j s", p=128)
        rot = [psh, pseg[0], pseg[1]]
        cnt = {"r": 0}

        def nrot():
            cnt["r"] += 1
            return rot[cnt["r"] % 3]

        def loads(c):
            r0 = c * 128
            a = c % 2
            K.dma(xs_c[a][:, :], T.xs_tm[r0:r0 + 128, :])
            K.dma(b_c[a][:, :], T.b_tm[r0:r0 + 128, :])
            K.dma(bcT_c[a][:, :, :], bcv[:, :, r0:r0 + 128])
            K.dma(dtda[a][:, 0:64], T.dt_tm[r0:r0 + 128, :])
            K.dma(dtda[a][:, 64:128], T.da_tm[r0:r0 + 128, :])
            K.dma(z_c[a][:, :], T.z_tm[r0:r0 + 128, :])
            K.dma(hb_c[a][:, :], T.hb[c])

        def front(c):
            a = c % 2
            da = dtda[a][:, 64:128]
            K.mm(pss[:, 0:64], cm[:, C_U:C_U + 128], da, sig=False)
            K.mm(pss[:, 64:128], cm[:, C_LS:C_LS + 128], da, sig=False)
            K.mm(pss[:, 128:192], cm[:, C_L:C_L + 128], da, sig=False)
            K.mm(pss[:, 192:256], cm[:, C_ONE:C_ONE + 128], da)
            K.actf(ew[a][:, :], pss[:, 0:256], AF.Exp)
            K.tt(wf[a][:, :], ew[a][:, 64:96], dtda[a][:, 0:32], ALU.mult)
            xs3 = _v3(xs_c[a][:, :], NH, HP)
            K.tt(_v3(xc_f[a][:, :], NH, HP), xs3, _bl(dtda[a][:, 0:32], HP), ALU.mult)
            K.tt(_v3(xc_b[a][:, :], NH, HP), xs3, _bl(dtda[a][:, 32:64], HP), ALU.mult, eng=K.pool)
            K.tt(_v3(xcs_f[a][:, :], NH, HP), xs3, _bl(wf[a][:, :], HP), ALU.mult, eng=K.pool)
            for g in range(4):
                K.mm(psh[:, g * 128:(g + 1) * 128], bcT_c[a][:, g, :], bcT_c[a][:, 4 + g, :], sig=(g == 3))
            K.tt(_v3(cbf[a][:, :], 4, 128), _v3(psh[:, :], 4, 128), _bm(cm[:, C_U:C_U + 128], 4), ALU.mult)
            K.tt(_v3(cbb[a][:, :], 4, 128), _v3(psh[:, :], 4, 128), _bm(cm[:, C_L:C_L + 128], 4), ALU.mult)

        def rgen(c, g):
            a = c % 2
            K.tt(_v3(Rg[0][g % 2][:, :], 8, 128), _bm(cm[:, C_U:C_U + 128], 8),
                 _bl(dtda[a][:, 64 + 8 * g:64 + 8 * g + 8], 128), ALU.mult)
            K.tt(_v3(Rg[1][g % 2][:, :], 8, 128), _bm(cm[:, C_L:C_L + 128], 8),
                 _bl(dtda[a][:, 96 + 8 * g:96 + 8 * g + 8], 128), ALU.mult)

        quads = [(g, dr, q) for g in range(4) for dr in range(2) for q in range(2)]

        def seg_stage(c, k):
            a = c % 2
            g, dr, q = quads[k]
            R = Rg[dr][g % 2]
            lmask = C_LS if dr == 0 else C_US
            cb = cbf[a] if dr == 0 else cbb[a]
            ps_ = pseg[k % 2]
            eb = ebuf[k % 3]
            m_ = mT[k % 3]
            K.mm(ps_[:, :], cm[:, lmask:lmask + 128], R[:, q * 512:(q + 1) * 512])
            K.actf(eb[:, :], ps_[:, :], AF.Exp)
            K.tt(_v3(m_[:, :], 4, 128), _v3(eb[:, :], 4, 128), _bm(cb[:, g * 128:(g + 1) * 128], 4), ALU.mult,
                 eng=(K.dve if k % 3 else K.pool))

        def y_stage(c, k):
            a = c % 2
            g, dr, q = quads[k]
            xc = xc_f[a] if dr == 0 else xc_b[a]
            h0 = g * 8 + q * 4
            m_ = mT[k % 3]
            for hh in range(4):
                h = h0 + hh
                first = (dr == 0 and q == 0 and hh == 0)
                last = (dr == 1 and q == 1 and hh == 3)
                K.mm(py[:, h * HP:(h + 1) * HP], m_[:, hh * 128:(hh + 1) * 128], xc[:, h * HP:(h + 1) * HP],
                     start=first, stop=last, sig=(hh == 3))

        loads(0)
        if NT > 1:
            loads(1)
        front(0)
        for c in range(NT):
            a = c % 2
            xs3 = _v3(xs_c[a][:, :], NH, HP)
            for g in range(4):
                gs = slice(g * 512, (g + 1) * 512)
                p1 = nrot()
                K.mm(p1[:, :], bcT_c[a][:, 4 + g, :], Hfb[a][:, gs])
                K.tt(_v3(yacc[:, gs], 8, HP), _v3(p1[:, :], 8, HP), _bl(ew[a][:, 8 * g:8 * g + 8], HP), ALU.mult)
                p2 = nrot()
                K.mm(p2[:, :], bcT_c[a][:, 4 + g, :], hb_c[a][:, gs])
                K.tt(_v3(ytmp[:, gs], 8, HP), _v3(p2[:, :], 8, HP), _bl(ew[a][:, 160 + 8 * g:160 + 8 * g + 8], HP), ALU.mult)
                K.tt(yacc[:, gs], yacc[:, gs], ytmp[:, gs], ALU.add, eng=K.pool)
            K.tt(_v3(Hf[:, :], NH, HP), _v3(Hf[:, :], NH, HP), _bl(ew[a][:, 192:224], HP), ALU.mult)
            for g in range(4):
                gs = slice(g * 512, (g + 1) * 512)
                p1 = nrot()
                K.mm(p1[:, :], b_c[a][:, g * 128:(g + 1) * 128], xcs_f[a][:, gs])
                K.tt(Hf[:, gs], Hf[:, gs], p1[:, :], ALU.add)
            K.cp(Hfb[(c + 1) % 2][:, :], Hf[:, :], eng=K.act)
            rgen(c, 0)
            seg_stage(c, 0)
            for k in range(16):
                if k % 4 == 0 and k // 4 + 1 < 4:
                    rgen(c, k // 4 + 1)
                if k + 1 < 16:
                    seg_stage(c, k + 1)
                y_stage(c, k)
                if k == 6 and c + 1 < NT:
                    front(c + 1)
            K.tt(yacc[:, :], yacc[:, :], py[:, :], ALU.add)
            K.tt(_v3(ytmp[:, :], NH, HP), xs3, _bl(dsk[:, :], HP), ALU.mult, eng=K.pool)
            K.tt(yacc[:, :], yacc[:, :], ytmp[:, :], ALU.add)
            K.actf(sz[:, :], z_c[a][:, :], AF.Silu)
            K.tt(yacc[:, :], yacc[:, :], sz[:, :], ALU.mult)
            K.actf(sz[:, :], yacc[:, :], AF.Square, accum=ssq[:, :])
            K.ts(ssq[:, :], ssq[:, :], 1.0 / DI, EPS, ALU.mult, ALU.add)
            K.actf(ssq[:, :], ssq[:, :], AF.Sqrt)
            K.recip(ssq[:, :], ssq[:, :])
            K.stt(yn[:, :], yacc[:, :], ssq[:, :], gssd[:, :], ALU.mult, ALU.mult)
            if c + 2 < NT:
                loads(c + 2)
            ci = c % 4
            for rnd in range(2):
                ptv = TV(pseg[rnd], pseg[rnd].h[:, :].bitcast(BF16)).rr("p (j t) -> p j t", j=8)
                for jj in range(8):
                    j = rnd * 8 + jj
                    K.tr(TV(ptv.tile, ptv.ap[:, jj, :]), yn[:, j * 128:(j + 1) * 128], identb[:, :], sig=(jj == 7))
                K.cp(yTs[:, rnd * 8:(rnd + 1) * 8, ci * 128:(ci + 1) * 128], ptv, eng=K.act)
            if ci == 3:
                K.dma(T.yssdT[:, (c - 3) * 128:(c + 1) * 128].rearrange("(j p) s -> p j s", p=128), yTs[:, :, :])
    K.barrier()


def phase_gqa(K, T, l, bg_items=None):
    import contextlib
    S = T.S
    NG = S // 512
    NT = S // 128
    NP = NT // 2
    with contextlib.ExitStack() as st:
        bg = wconv_gen(K, T, st, bg_items, [K.dve], NB=4) if bg_items else None
        onesf = K.sb(st, "onesf", [128, 128], F32)
        K.mset(onesf[:, :], 1.0)
        ones = K.sb(st, "ones", [128, 128], BF16)
        K.cp(ones[:, :], onesf[:, :])
        kT = [K.sb(st, f"kT{i}", [128, S], BF16) for i in range(2)]
        vg = [K.sb(st, f"vg{i}", [128, NT, 128], BF16) for i in range(2)]
        qT = [K.sb(st, f"qT{i}", [128, S], BF16) for i in range(2)]
        pS = [K.ps(st, f"pS{i}", [128, 1024], F32) for i in range(3)]
        pO = [K.ps(st, f"pO{i}", [128, 512], F32) for i in range(1)]
        pD = [K.ps(st, f"pD{i}", [128, 512], F32) for i in range(1)]
        NPT = 6
        pt = [K.sb(st, f"pt{i}", [128, 1024], BF16) for i in range(NPT)]
        psm = [K.sb(st, f"psm{i}", [128, 512], BF16) for i in range(6)]
        NQS = 6
        qsm = [K.sb(st, f"qsm{i}", [128, 512], BF16) for i in range(NQS)]
        rinv = [K.sb(st, f"rinv{i}", [128, 512], F32) for i in range(2)]
        oT = [K.sb(st, f"oT{i}", [128, 512], BF16) for i in range(2)]
        vv = T.v_tm.rearrange("(j p) d -> p j d", p=128)

        def load_kv(g):
            K.dma(kT[g % 2][:, :], T.kT[g * 128:(g + 1) * 128, :])
            step = 16
            for j0 in range(0, NT, step):
                K.dma(vg[g % 2][:, j0:j0 + step, :], vv[:, j0:j0 + step, g * 128:(g + 1) * 128])

        def load_q(h):
            K.dma(qT[h % 2][:, :], T.qT[h * 128:(h + 1) * 128, :])

        load_kv(0)
        load_q(0)
        n = 0
        it = 0
        nq = 0
        fin_state = {"f": None}
        for g in range(4):
            if g + 1 < 4:
                load_kv(g + 1)
            k_, v_ = kT[g % 2], vg[g % 2]
            for r in range(4):
                h = g * 4 + r
                if h + 1 < 16:
                    load_q(h + 1)
                q_ = qT[h % 2]
                warm(K, pS[n % 3][:, 0:512], ones[:, :], q_[:, 0:512], n=(16 if r == 0 else 10))
                for qg in range(NG):
                    if bg is not None:
                        next(bg, None)
                    po, pd = pO[0], pD[0]
                    qs = q_[:, qg * 512:(qg + 1) * 512]
                    slot = {}
                    pend_den = []
                    nden = NP // 2
                    for p in range(NP + 2):
                        if p < NP:
                            cur = n % NPT
                            ps_ = pS[n % 3]
                            n += 1
                            slot[p] = cur
                            K.mm(ps_[:, 0:512], k_[:, (2 * p) * 128:(2 * p + 1) * 128], qs)
                            K.mm(ps_[:, 512:1024], k_[:, (2 * p + 1) * 128:(2 * p + 2) * 128], qs)
                            K.actf(pt[cur][:, :], ps_[:, :], AF.Exp, scale=ATT_SCALE)
                        while pend_den and pend_den[0][2] <= p:
                            m_, t_, _ = pend_den.pop(0)
                            K.mm(pd[:, :], ones[:, :], t_[:, :], start=(m_ == 0), stop=(m_ == nden - 1))
                        pp = p - 2
                        if pp == 0 and fin_state["f"] is not None:
                            fin_state["f"]()
                            fin_state["f"] = None
                        if 0 <= pp < NP:
                            c_ = slot[pp]
                            K.mm(po[:, :], v_[:, 2 * pp, :], pt[c_][:, 0:512], start=(pp == 0), stop=False, sig=False)
                            K.mm(po[:, :], v_[:, 2 * pp + 1, :], pt[c_][:, 512:1024], start=False, stop=(pp == NP - 1))
                            eng = K.dve if pp % 2 == 0 else K.pool
                            K.tt(psm[pp % 6][:, :], pt[c_][:, 0:512], pt[c_][:, 512:1024], ALU.add, eng=eng)
                            if pp % 2 == 1:
                                m_ = pp // 2
                                t_ = qsm[nq % NQS]
                                nq += 1
                                K.tt(t_[:, :], psm[(pp - 1) % 6][:, :], psm[pp % 6][:, :], ALU.add, eng=K.dve)
                                pend_den.append((m_, t_, p + 4))
                    def finish(pend_den=pend_den, it=it, h=h, qg=qg, po=po, pd=pd, nden=nden):
                        for m_, t_, _ in pend_den:
                            K.mm(pd[:, :], ones[:, :], t_[:, :], start=(m_ == 0), stop=(m_ == nden - 1))
                        K.recip(rinv[it % 2][:, :], pd[:, :])
                        K.tt(oT[it % 2][:, :], po[:, :], rinv[it % 2][:, :], ALU.mult)
                        K.dma(T.ygqaT[h * 128:(h + 1) * 128, qg * 512:(qg + 1) * 512], oT[it % 2][:, :])

                    fin_state["f"] = finish
                    it += 1
        if fin_state["f"] is not None:
            fin_state["f"]()
        if bg is not None:
            for _ in bg:
                pass
    K.barrier()


def phase_dil(K, T, l):
    import contextlib
    S = T.S
    with contextlib.ExitStack() as st:
        cm = K.sb(st, "cm", [128, CW], F32)
        K.dma(cm[:, :], T.cmat[:, :])
        onesf = K.sb(st, "onesf", [128, 128], F32)
        K.mset(onesf[:, :], 1.0)
        ones = K.sb(st, "ones", [128, 128], BF16)
        K.cp(ones[:, :], onesf[:, :])
        rb = K.sb(st, "rb", [128, 384], F32)
        load_bcast(K, rb[:, :], T.v["rel_bias"].rearrange("a b -> (a b)"))
        Ttab = K.sb(st, "Ttab", [128, 12, 384], F32)
        tmpb = [K.sb(st, f"tmpb{i}", [128, 384], F32) for i in range(2)]
        if False:
            pass
        else:
            K.dma(Ttab[:, :, :], T.ttab.rearrange("p (h c) -> p h c", h=12))
        qT = K.sb(st, "qT", [128, S], BF16)
        kT = K.sb(st, "kT", [128, S], BF16)
        vrs = [K.sb(st, f"vr{i}", [128, S // 128, 128], BF16) for i in range(2)]
        num = K.sb(st, "num", [128, S], F32)
        den = K.sb(st, "den", [128, S], F32)
        pS = [K.ps(st, f"pS{i}", [128, 512], F32) for i in range(2)]
        qTr = K.sb(st, "qTr", [128, S], BF16)
        kTr = K.sb(st, "kTr", [128, S], BF16)
        pO = [K.ps(st, f"pO{i}", [128, 512], F32) for i in range(2)]
        pD = [K.ps(st, f"pD{i}", [128, 512], F32) for i in range(2)]
        sb_ = [K.sb(st, f"sb{i}", [128, 384], F32) for i in range(3)]
        pt = [K.sb(st, f"pt{i}", [128, 384], BF16) for i in range(3)]
        ost = [K.sb(st, f"ost{i}", [128, 2048], BF16) for i in range(2)]
        n = 0
        nb_ = 0
        heads = [(j, g) for j in range(4) for g in range(3)]

        def load_head(idx):
            j_, g_ = heads[idx]
            hd_ = g_ * 4 + j_
            dd_ = DILS[g_]
            NBr_ = S // dd_ // 128
            K.dma(qT[:, :], T.dqT[hd_ * 128:(hd_ + 1) * 128, :])
            K.dma(kT[:, :], T.dkT[hd_ * 128:(hd_ + 1) * 128, :])
            for r in range(dd_):
                src = bass.AP(tensor=T.dv_tm.tensor, offset=T.dv_tm.offset + r * DW + hd_ * 128,
                              ap=[[dd_ * DW, 128], [128 * dd_ * DW, NBr_], [1, 128]])
                K.dma(vrs[idx % 2][:, r * NBr_:(r + 1) * NBr_, :], src)

        load_head(0)
        for hidx, (j, g) in enumerate(heads):
            if True:
                hd = g * 4 + j
                dd = DILS[g]
                L = S // dd
                NBr = L // 128
                vr = vrs[hidx % 2]
                nb = min(4, NBr)
                qm, km = qTr, kTr
                for r in range(dd):
                    K.cp(qTr[:, r * L:(r + 1) * L], qT[:, 0:L].mod(1, stride=dd, off=r), eng=K.dve)
                    K.cp(kTr[:, r * L:(r + 1) * L], kT[:, 0:L].mod(1, stride=dd, off=r), eng=K.pool)
                if hidx + 1 < len(heads):
                    load_head(hidx + 1)

                def mcols(t, r, i0, cnt):
                    return t[:, r * L + 128 * i0:r * L + 128 * i0 + cnt]

                def cols(t, r, i0, cnt):
                    return t[:, 0:cnt].mod(1, stride=dd, off=r + dd * 128 * i0)

                items = []
                for r in range(dd):
                    for ib in range(0, NBr, nb):
                        bank = nb_
                        nb_ += 1
                        for ii in range(nb):
                            items.append((r, ib, ii, bank))

                def stage_a(k):
                    r, ib, ii, bank = items[k]
                    i = ib + ii
                    ps_ = pS[k % 2]
                    s_ = sb_[k % 3]
                    p_ = pt[k % 3]
                    os_ = [o for o in range(3) if 0 <= i - 1 + o < NBr]
                    lo, hi = os_[0] * 128, (os_[-1] + 1) * 128
                    for o in os_:
                        kb = i - 1 + o
                        K.mm(ps_[:, o * 128:(o + 1) * 128], mcols(km, r, kb, 128), mcols(qm, r, i, 128),
                             sig=(o == os_[-1]))
                    K.stt(s_[:, lo:hi], ps_[:, lo:hi], ATT_SCALE, Ttab[:, hd, lo:hi], ALU.mult, ALU.add)
                    K.actf(p_[:, lo:hi], s_[:, lo:hi], AF.Exp)

                def stage_b(k):
                    r, ib, ii, bank = items[k]
                    i = ib + ii
                    po, pd = pO[bank % 2], pD[bank % 2]
                    p_ = pt[k % 3]
                    os_ = [o for o in range(3) if 0 <= i - 1 + o < NBr]
                    for o in os_:
                        kb = i - 1 + o
                        K.mm(po[:, ii * 128:(ii + 1) * 128], vr[:, r * NBr + kb, :], p_[:, o * 128:(o + 1) * 128],
                             start=(o == os_[0]), stop=(o == os_[-1]), sig=False)
                    for o in os_:
                        K.mm(pd[:, ii * 128:(ii + 1) * 128], ones[:, :], p_[:, o * 128:(o + 1) * 128],
                             start=(o == os_[0]), stop=(o == os_[-1]), sig=(o == os_[-1]))
                    if ii == nb - 1:
                        w = nb * 128
                        nv = cols(num, r, ib, w)
                        dv_ = cols(den, r, ib, w)
                        if g == 0:
                            K.cp(nv, po[:, :w], eng=K.act)
                            K.cp(dv_, pd[:, :w], eng=K.dve)
                        else:
                            K.tt(nv, nv, po[:, :w], ALU.add)
                            K.tt(dv_, dv_, pd[:, :w], ALU.add, eng=K.dve)

                stage_a(0)
                for k in range(len(items)):
                    if k + 1 < len(items):
                        stage_a(k + 1)
                    stage_b(k)
            for c0 in (range(0, S, 2048) if g == 2 else ()):
                K.recip(den[:, c0:c0 + 2048], den[:, c0:c0 + 2048])
                o_ = ost[(c0 // 2048) % 2]
                K.tt(o_[:, :], num[:, c0:c0 + 2048], den[:, c0:c0 + 2048], ALU.mult)
                K.dma(T.ydilT[j * 128:(j + 1) * 128, c0:c0 + 2048], o_[:, :])
    K.barrier()


def warm(K, ps_tv, lhsT, rhs, n=16):
    for i in range(n):
        K.mm(ps_tv, lhsT, rhs, start=True, stop=True, sig=(i == n - 1))


class TVslice:
    def __init__(self, tile, c0, c1):
        self.tile, self.c0, self.c1 = tile, c0, c1

    def __getitem__(self, idx):
        return self.tile[:, self.c0:self.c1]


class WStream:
    def __init__(self, K, slots, plan):
        self.K, self.slots, self.plan = K, slots, plan
        self.i = 0
        self.n = len(slots)
        for idx in range(min(self.n - 1, len(plan))):
            self._load(idx)

    def _load(self, idx):
        ap, kc0, nkc, c0, w = self.plan[idx]
        slot = self.slots[idx % self.n]
        self.K.dma(slot[:, 0:nkc, 0:w], ap.rearrange("(kc p) n -> p kc n", p=128)[:, kc0:kc0 + nkc, c0:c0 + w])

    def next(self, ap=None):
        idx = self.i
        self.i += 1
        if ap is not None:
            assert self.plan[idx][0] is ap, "weight stream plan mismatch"
        if idx + self.n - 1 < len(self.plan):
            self._load(idx + self.n - 1)
        return self.slots[idx % self.n]


def phase_mix(K, T, l, src_x, dst_x):
    import contextlib
    S = T.S
    NG = S // 512
    W = {k: T.wb[k][l] for k in T.wb}
    with contextlib.ExitStack() as st:
        idf = K.sb(st, "idf", [128, 128], F32)
        K.dma(idf[:, :], T.cmat[:, C_ID:C_ID + 128])
        identb = K.sb(st, "identb", [128, 128], BF16)
        K.cp(identb[:, :], idf[:, :])
        gains = {}
        for nm in ("g_post_mix", "g_pre_ffn", "g_post_ffn", "g_ple"):
            gains[nm] = K.sb(st, nm, [128, D], F32)
            load_bcast(K, gains[nm][:, :], T.v[nm][l])
        wsl = [K.sb(st, f"wsl{i}", [128, 16, 512], BF16) for i in range(3)]
        U = K.sb(st, "U", [128, 36, 512], BF16)
        aT = K.sb(st, "aT", [128, FT, 512], BF16)
        gt = K.sb(st, "gt", [128, 12, 512], BF16)
        xt = [K.sb(st, f"xt{i}", [128, D], F32) for i in range(4)]
        mo = [K.sb(st, f"mo{i}", [128, D], F32) for i in range(4)]
        hT = K.sb(st, "hT", [128, 8, 512], BF16)
        mixT = hT
        pl = K.sb(st, "pl", [128, 4, PLE], F32)
        plb = K.sb(st, "plb", [128, 4, PLE], BF16)
        pTt = K.sb(st, "pTt", [128, 2, 512], BF16)
        sg = K.sb(st, "sg", [128, 3, 512], F32)
        sgl = [K.sb(st, f"sgl{i}", [128, 512], F32) for i in range(4)]
        xn = [K.sb(st, f"xn{i}", [128, D], BF16) for i in range(2)]
        ssx = [K.sb(st, f"ssx{i}", [128, 1], F32) for i in range(2)]
        pb = [K.ps(st, f"pb{i}", [128, 512], F32) for i in range(6)]
        pT = [K.ps(st, f"pT{i}", [128, 8, 128], BF16) for i in range(2)]
        junk = sg[:, 0:2, :]
        macc = [TVslice(mo[i], 0, 512) for i in range(4)]
        cnt = {"ev": 0, "ss": 0, "pb": 0}

        plan1 = []
        for nb in range(2):
            plan1 += [(W["w_br_ssd"], 0, 16, nb * 512, 512), (W["w_br_gqa"], 0, 16, nb * 512, 512),
                      (W["w_br_dil"], 0, 4, nb * 512, 512)]
        plan1 += [(W["w_out"], 0, 8, 0, 512), (W["w_out"], 0, 8, 512, 512)]
        for fb in range(6):
            w = 512 if fb < 5 else 256
            plan1 += [(W["w_gate"], 0, 8, fb * 512, w), (W["w_up"], 0, 8, fb * 512, w)]
        for nb in range(2):
            plan1 += [(W["w_down"], 0, 16, nb * 512, 512), (W["w_down"], 16, 6, nb * 512, 512)]
        for nb in range(2):
            plan1 += [(W["w_ple_gate"], 0, 8, nb * 512, 512), (W["w_ple"], 0, 2, nb * 512, 512)]
        ws = WStream(K, wsl, plan1 * NG)

        def evac(out, in_):
            cnt["ev"] += 1
            K.cp(out, in_, eng=(K.act if cnt["ev"] % 2 else K.dve))

        def rms(src):
            s_ = ssx[cnt["ss"] % 2]
            cnt["ss"] += 1
            K.actf(junk, src.rr("p (a b) -> p a b", a=2) if False else _v3(src, 2, 512), AF.Square, accum=s_[:, :])
            K.ts(s_[:, :], s_[:, :], 1.0 / D, EPS, ALU.mult, ALU.add)
            K.actf(s_[:, :], s_[:, :], AF.Sqrt)
            K.recip(s_[:, :], s_[:, :])
            return s_

        def nextpb():
            cnt["pb"] += 1
            return pb[cnt["pb"] % 6]

        def norm_T(i, gain):
            s_ = rms(xt[i][:, :])
            K.stt(xn[i % 2][:, :], xt[i][:, :], s_[:, :], gain[:, :], ALU.mult, ALU.mult)
            pt = pT[i % 2]
            for kc in range(KD):
                K.tr(pt[:, kc, :], xn[i % 2][:, kc * 128:(kc + 1) * 128], identb[:, :], sig=(kc == KD - 1))
            evac(hT[:, :, i * 128:(i + 1) * 128], pt[:, :, :])

        def post_norm_add(i, gain):
            s_ = rms(mo[i][:, :])
            K.stt(mo[i][:, :], mo[i][:, :], s_[:, :], gain[:, :], ALU.mult, ALU.mult)
            K.tt(xt[i][:, :], xt[i][:, :], mo[i][:, :], ALU.add)

        def load_U(tg_):
            t_ = tg_ * 512
            K.dma(U[:, 0:16, :], T.yssdT[:, t_:t_ + 512].rearrange("(j p) s -> p j s", p=128))
            K.dma(U[:, 16:32, :], T.ygqaT[:, t_:t_ + 512].rearrange("(j p) s -> p j s", p=128))
            K.dma(U[:, 32:36, :], T.ydilT[:, t_:t_ + 512].rearrange("(j p) s -> p j s", p=128))

        for tg in range(NG):
            t0 = tg * 512
            if tg == 0:
                load_U(0)
            for i in range(4):
                K.dma(xt[i][:, :], src_x[t0 + i * 128:t0 + (i + 1) * 128, :])
            K.dma(pl[:, :, :], T.p[l][t0:t0 + 512, :].rearrange("(i p) c -> p i c", p=128))
            warm(K, pb[0][:, :], identb[:, :], U[:, 0, :], n=12)
            for nb in range(2):
                for b in range(3):
                    r0 = b * D + nb * 512
                    K.dma(gt[:, b * 4:(b + 1) * 4, :], T.gatesT[r0:r0 + 512, t0:t0 + 512].rearrange("(j p) s -> p j s", p=128))
                for b, (wname, nkc, u0) in enumerate((("w_br_ssd", 16, 0), ("w_br_gqa", 16, 16), ("w_br_dil", 4, 32))):
                    blk = ws.next(W[wname])
                    for jj in range(4):
                        nt = nb * 4 + jj
                        pA = nextpb()
                        for kc in range(nkc):
                            K.mm(pA[:, :], blk[:, kc, jj * 128:(jj + 1) * 128], U[:, u0 + kc, :], start=(kc == 0),
                                 stop=(kc == nkc - 1), sig=(kc == nkc - 1))
                        sl = sgl[(b * 4 + jj) % 4]
                        K.actf(sl[:, :], gt[:, b * 4 + jj, :], AF.Sigmoid)
                        if b == 0:
                            K.tt(macc[jj][:, :], pA[:, :], sl[:, :], ALU.mult)
                        else:
                            K.tt(sl[:, :], pA[:, :], sl[:, :], ALU.mult)
                            if b == 1:
                                K.tt(macc[jj][:, :], macc[jj][:, :], sl[:, :], ALU.add, eng=K.pool)
                            else:
                                K.tt(mixT[:, nt, :], macc[jj][:, :], sl[:, :], ALU.add, eng=K.pool)
            if tg + 1 < NG:
                load_U(tg + 1)
            for nb in range(2):
                bo = ws.next(W["w_out"])
                for i in range(4):
                    po = nextpb()
                    for kc in range(KD):
                        K.mm(po[:, :], mixT[:, kc, i * 128:(i + 1) * 128], bo[:, kc, :], start=(kc == 0), stop=(kc == KD - 1), sig=(kc == KD - 1))
                    evac(mo[i][:, nb * 512:(nb + 1) * 512], po[:, :])
            if T.debug and tg == 0:
                K.dma(T.dbg_mix.rearrange("(j p) s -> p j s", p=128), mixT[:, :, :])
                for i in range(4):
                    K.dma(T.dbg_mo[i * 128:(i + 1) * 128, :], mo[i][:, :])
            for i in range(4):
                post_norm_add(i, gains["g_post_mix"])
            if T.debug and tg == 0:
                for i in range(4):
                    K.dma(T.dbg_xa[i * 128:(i + 1) * 128, :], xt[i][:, :])
            for i in range(4):
                norm_T(i, gains["g_pre_ffn"])
            warm(K, pb[0][:, :], identb[:, :], hT[:, 0, :], n=12)
            for fb in range(6):
                w = 512 if fb < 5 else 256
                bgt = ws.next(W["w_gate"])
                for jj in range(w // 128):
                    pG = nextpb()
                    for kc in range(KD):
                        K.mm(pG[:, :], bgt[:, kc, jj * 128:(jj + 1) * 128], hT[:, kc, :], start=(kc == 0), stop=(kc == KD - 1), sig=(kc == KD - 1))
                    K.actf(sgl[jj][:, :], pG[:, :], AF.Silu)
                bu = ws.next(W["w_up"])
                for jj in range(w // 128):
                    ft = fb * 4 + jj
                    pU = nextpb()
                    for kc in range(KD):
                        K.mm(pU[:, :], bu[:, kc, jj * 128:(jj + 1) * 128], hT[:, kc, :], start=(kc == 0), stop=(kc == KD - 1), sig=(kc == KD - 1))
                    K.tt(aT[:, ft, :], sgl[jj][:, :], pU[:, :], ALU.mult)
            for nb in range(2):
                b1 = ws.next(W["w_down"])
                for i in range(4):
                    for fc in range(16):
                        K.mm(pb[i][:, :], aT[:, fc, i * 128:(i + 1) * 128], b1[:, fc, :], start=(fc == 0), stop=False, sig=(fc == 15))
                b2 = ws.next(W["w_down"])
                for i in range(4):
                    for fc in range(6):
                        K.mm(pb[i][:, :], aT[:, 16 + fc, i * 128:(i + 1) * 128], b2[:, fc, :], start=False, stop=(fc == 5), sig=(fc == 5))
                    evac(mo[i][:, nb * 512:(nb + 1) * 512], pb[i][:, :])
            for i in range(4):
                post_norm_add(i, gains["g_post_ffn"])
            if T.debug and tg == 0:
                for i in range(4):
                    K.dma(T.dbg_xb[i * 128:(i + 1) * 128, :], xt[i][:, :])
            for i in range(4):
                norm_T(i, gains["g_ple"])
            K.cp(plb[:, :, :], pl[:, :, :], eng=K.pool)
            for i in range(4):
                pt = pT[i % 2]
                for kc in range(2):
                    K.tr(pt[:, kc, :], plb[:, i, kc * 128:(kc + 1) * 128], identb[:, :], sig=(kc == 1))
                evac(pTt[:, :, i * 128:(i + 1) * 128], pt[:, 0:2, :])
            warm(K, pb[5][:, :], identb[:, :], hT[:, 0, :], n=12)
            for nb in range(2):
                hs = slice(nb * 512, (nb + 1) * 512)
                bpg = ws.next(W["w_ple_gate"])
                for i in range(4):
                    for kc in range(KD):
                        K.mm(pb[i][:, :], hT[:, kc, i * 128:(i + 1) * 128], bpg[:, kc, :], start=(kc == 0), stop=(kc == KD - 1), sig=(kc == KD - 1))
                    K.actf(mo[i][:, hs], pb[i][:, :], AF.Sigmoid)
                bp = ws.next(W["w_ple"])
                for i in range(4):
                    po = pb[4 + i % 2]
                    for kc in range(2):
                        K.mm(po[:, :], pTt[:, kc, i * 128:(i + 1) * 128], bp[:, kc, :], start=(kc == 0), stop=(kc == 1), sig=(kc == 1))
                    K.tt(mo[i][:, hs], mo[i][:, hs], po[:, :], ALU.mult)
                    K.tt(xt[i][:, hs], xt[i][:, hs], mo[i][:, hs], ALU.add, eng=K.pool)
            for i in range(4):
                K.dma(dst_x[t0 + i * 128:t0 + (i + 1) * 128, :], xt[i][:, :])
    K.barrier()


SEQ = 8192
N_CORES = 8
_CACHE = {}


def kernel(x_prompt, x_sample, p_prompt, p_sample, **W):
    S = SEQ
    if "nc" not in _CACHE:
        _CACHE["nc"] = build(S)[0]
        _CACHE["consts"] = make_consts(S)
    nc = _CACHE["nc"]
    consts = _CACHE["consts"]
    W = {k: np.ascontiguousarray(np.asarray(v), dtype=np.float32) for k, v in W.items()}
    x_prompt = np.asarray(x_prompt)
    x_sample = np.asarray(x_sample)
    p_prompt = np.asarray(p_prompt)
    p_sample = np.asarray(p_sample)
    nb = x_prompt.shape[0]
    ns = x_sample.shape[0]
    seqs = [("p", i) for i in range(nb)] + [("s", i) for i in range(ns)]
    assert len(seqs) <= N_CORES
    in_maps = []
    for c in range(N_CORES):
        kind, i = seqs[c % len(seqs)]
        if kind == "p":
            x, p = x_prompt[i], p_prompt[:, i]
        else:
            x, p = x_sample[i], p_sample[:, i]
        in_maps.append(core_inputs(S, x, p, W, consts))
    res = run_bass_kernel_spmd(nc, in_maps, core_ids=list(range(N_CORES)))
    ys = [np.asarray(r["y"], dtype=np.float32) for r in res.results]
    y_prompt = np.stack(ys[:nb], axis=0)
    y_sample = np.stack(ys[nb:nb + ns], axis=0)
    return (y_prompt, y_sample)
```
